# Optimizing a Trainium2 kernel written in Bass

```python
import math
import jax, jax.numpy as jnp
from jax import lax
import numpy as np

D_MODEL = 2048
BATCH = 2
SEQ = 4096
DEPTH = 2
DEC_BATCH = 16
DEC_SEQ = 32
PAST_LEN = 1024

CHUNK = 64
N_MIXERS = 2
N_CONV_LAYERS = (DEPTH + N_MIXERS - 1) // N_MIXERS
N_ATTN_LAYERS = DEPTH // N_MIXERS
CONV_K = 31
N_HEADS = 32
N_KV_HEADS = 4
HEAD_DIM = 64
GROUP = N_HEADS // N_KV_HEADS
WINDOW = 128
N_BUCKETS = 32
MAX_DISTANCE = 128
D_FF = 5632
FFN_K = 3
EPS = 1e-6
NEG_INF = -1e30
ATTN_SCALE = HEAD_DIM ** -0.5

kernel_name = "streaming_conformer_swa_hybrid_step"


def rmsnorm(x, g):
    xf = x.astype(jnp.float32)
    y = xf * lax.rsqrt(jnp.mean(xf * xf, axis=-1, keepdims=True) + EPS)
    return (y * g.astype(jnp.float32)).astype(x.dtype)


def layernorm(x, g, b):
    xf = x.astype(jnp.float32)
    mu = jnp.mean(xf, axis=-1, keepdims=True)
    var = jnp.mean(jnp.square(xf - mu), axis=-1, keepdims=True)
    y = (xf - mu) * lax.rsqrt(var + EPS) * g.astype(jnp.float32) + b.astype(jnp.float32)
    return y.astype(x.dtype)


def modulate(h, shift, scale):
    return h * (1 + scale[:, None, :]) + shift[:, None, :]


def causal_dwconv(u, past, w, b):
    width = w.shape[0]
    full = jnp.concatenate([past.astype(u.dtype), u], axis=1)
    y = lax.conv_general_dilated(full, w[:, None, :].astype(u.dtype), (1,), 'VALID',
                                 dimension_numbers=('NWC', 'WIO', 'NWC'),
                                 feature_group_count=u.shape[-1])
    return y + b, full[:, -(width - 1):]


def t5_bucket(rel):
    half = N_BUCKETS // 2
    max_exact = half // 2
    n = jnp.abs(rel)
    ret = jnp.where(rel > 0, half, 0)
    nf = jnp.maximum(n, 1).astype(jnp.float32)
    large = max_exact + (jnp.log(nf / max_exact) / math.log(MAX_DISTANCE / max_exact)
                         * (half - max_exact)).astype(jnp.int32)
    large = jnp.minimum(large, half - 1)
    return ret + jnp.where(n < max_exact, n, large)


def conformer_conv(h, past, w_in, b_in, w_dw, b_dw, ln_g, ln_b, w_out, b_out):
    a, g = jnp.split(h @ w_in + b_in, 2, axis=-1)
    u = a * jax.nn.sigmoid(g)
    y, new_past = causal_dwconv(u, past, w_dw, b_dw)
    y = jax.nn.silu(layernorm(y, ln_g, ln_b))
    return y @ w_out + b_out, new_past


def conv_glu_ffn(h, past, w_up, w_dw, b_dw, w_down):
    g, v = jnp.split(h @ w_up, 2, axis=-1)
    g, new_past = causal_dwconv(g, past, w_dw, b_dw)
    return (jax.nn.gelu(g) * v) @ w_down, new_past


def swa_attention(h, past_k, past_v, past_valid, keep, w_q, w_k, w_v, w_o, sinks, rel_bias):
    B, T, _ = h.shape
    P = past_k.shape[1]
    q = (h @ w_q).reshape(B, T, N_KV_HEADS, GROUP, HEAD_DIM)
    k = (h @ w_k).reshape(B, T, N_KV_HEADS, HEAD_DIM)
    v = (h @ w_v).reshape(B, T, N_KV_HEADS, HEAD_DIM)
    k_full = jnp.concatenate([past_k.astype(k.dtype), k], axis=1)
    v_full = jnp.concatenate([past_v.astype(v.dtype), v], axis=1)
    Q = min(T, CHUNK)
    N = T // Q
    K = P + Q
    qb = q.reshape(B, N, Q, N_KV_HEADS, GROUP, HEAD_DIM)
    key_idx = jnp.arange(N)[:, None] * Q + jnp.arange(K)[None, :]
    kb = k_full[:, key_idx]
    vb = v_full[:, key_idx]
    rel = jnp.arange(K)[None, :] - P - jnp.arange(Q)[:, None]
    bias = rel_bias[t5_bucket(rel)].astype(jnp.float32)
    bias = jnp.transpose(bias, (2, 0, 1)).reshape(N_KV_HEADS, GROUP, Q, K)
    valid = (key_idx - P) >= -past_valid
    s = jnp.einsum('bnqhgd,bnkhd->bnhgqk', qb, kb,
                   preferred_element_type=jnp.float32) * ATTN_SCALE + bias
    s = jnp.where(valid[None, :, None, None, None, :], s, NEG_INF)
    sink = sinks.astype(jnp.float32).reshape(N_KV_HEADS, GROUP)[None, None, :, :, None, None]
    mx = jnp.maximum(jnp.max(s, axis=-1, keepdims=True), sink)
    p = jnp.exp(s - mx)
    p = p / (jnp.sum(p, axis=-1, keepdims=True) + jnp.exp(sink - mx))
    o = jnp.einsum('bnhgqk,bnkhd->bnqhgd', p, vb.astype(jnp.float32))
    o = o.astype(h.dtype).reshape(B, T, N_HEADS * HEAD_DIM)
    return o @ w_o, k_full[:, -keep:], v_full[:, -keep:]


def setup_inputs(seed: int = 0) -> dict:
    key = jax.random.key(seed)
    ks = iter(jax.random.split(key, 40))
    D = D_MODEL
    win_cache = min(WINDOW, PAST_LEN)

    def nrm(shape, scale):
        return jax.random.normal(next(ks), shape, jnp.float32) * scale

    return {
        "x_prompt": nrm((BATCH, SEQ, D), 1.0),
        "x_sample": nrm((DEC_BATCH, DEC_SEQ, D), 1.0),
        "c_prompt": nrm((BATCH, D), 1.0),
        "c_sample": nrm((DEC_BATCH, D), 1.0),
        "cache_conv": nrm((N_CONV_LAYERS, DEC_BATCH, CONV_K - 1, D), 0.5),
        "cache_k": nrm((N_ATTN_LAYERS, DEC_BATCH, win_cache, N_KV_HEADS, HEAD_DIM), 1.0),
        "cache_v": nrm((N_ATTN_LAYERS, DEC_BATCH, win_cache, N_KV_HEADS, HEAD_DIM), 1.0),
        "cache_ffn": nrm((DEPTH, DEC_BATCH, FFN_K - 1, D_FF), 1.0),
        "w_mod": nrm((DEPTH, D, 6 * D), 0.5 * D ** -0.5),
        "b_mod": nrm((DEPTH, 6 * D), 0.02),
        "g_norm": 1.0 + nrm((DEPTH, 4, D), 0.05),
        "conv_w_in": nrm((N_CONV_LAYERS, D, 2 * D), D ** -0.5),
        "conv_b_in": nrm((N_CONV_LAYERS, 2 * D), 0.02),
        "conv_w_dw": nrm((N_CONV_LAYERS, CONV_K, D), CONV_K ** -0.5),
        "conv_b_dw": nrm((N_CONV_LAYERS, D), 0.02),
        "conv_ln_g": 1.0 + nrm((N_CONV_LAYERS, D), 0.05),
        "conv_ln_b": nrm((N_CONV_LAYERS, D), 0.02),
        "conv_w_out": nrm((N_CONV_LAYERS, D, D), D ** -0.5),
        "conv_b_out": nrm((N_CONV_LAYERS, D), 0.02),
        "attn_w_q": nrm((N_ATTN_LAYERS, D, N_HEADS * HEAD_DIM), D ** -0.5),
        "attn_w_k": nrm((N_ATTN_LAYERS, D, N_KV_HEADS * HEAD_DIM), D ** -0.5),
        "attn_w_v": nrm((N_ATTN_LAYERS, D, N_KV_HEADS * HEAD_DIM), D ** -0.5),
        "attn_w_o": nrm((N_ATTN_LAYERS, N_HEADS * HEAD_DIM, D), (N_HEADS * HEAD_DIM) ** -0.5),
        "attn_sinks": nrm((N_ATTN_LAYERS, N_HEADS), 0.5),
        "rel_bias": nrm((N_BUCKETS, N_HEADS), 0.3),
        "ffn_w_up": nrm((DEPTH, D, 2 * D_FF), D ** -0.5),
        "ffn_w_dw": nrm((DEPTH, FFN_K, D_FF), FFN_K ** -0.5),
        "ffn_b_dw": nrm((DEPTH, D_FF), 0.02),
        "ffn_w_down": nrm((DEPTH, D_FF, D), D_FF ** -0.5),
    }


def reference(x_prompt, x_sample, c_prompt, c_sample, cache_conv, cache_k, cache_v, cache_ffn,
              w_mod, b_mod, g_norm,
              conv_w_in, conv_b_in, conv_w_dw, conv_b_dw, conv_ln_g, conv_ln_b, conv_w_out, conv_b_out,
              attn_w_q, attn_w_k, attn_w_v, attn_w_o, attn_sinks, rel_bias,
              ffn_w_up, ffn_w_dw, ffn_b_dw, ffn_w_down):
    xs = [x_prompt, x_sample]
    cs = [c_prompt, c_sample]
    conv_new = [[], []]
    k_new = [[], []]
    v_new = [[], []]
    ffn_new = [[], []]
    for i in range(DEPTH):
        j = i // N_MIXERS
        for grp in range(2):
            x = xs[grp]
            Bg, T = x.shape[0], x.shape[1]
            sh1, sc1, ga1, sh2, sc2, ga2 = jnp.split(
                jax.nn.silu(cs[grp]) @ w_mod[i] + b_mod[i], 6, axis=-1)
            h = modulate(rmsnorm(x, g_norm[i, 0]), sh1, sc1)
            if i % N_MIXERS == 0:
                past = jnp.zeros((Bg, CONV_K - 1, D_MODEL), x.dtype) if grp == 0 else cache_conv[j]
                m, st = conformer_conv(h, past, conv_w_in[j], conv_b_in[j], conv_w_dw[j], conv_b_dw[j],
                                       conv_ln_g[j], conv_ln_b[j], conv_w_out[j], conv_b_out[j])
                conv_new[grp].append(st)
            else:
                if grp == 0:
                    past_k = jnp.zeros((Bg, WINDOW, N_KV_HEADS, HEAD_DIM), x.dtype)
                    past_v = past_k
                    past_valid = 0
                    keep = min(WINDOW, T)
                else:
                    past_k = cache_k[j]
                    past_v = cache_v[j]
                    past_valid = past_k.shape[1]
                    keep = past_k.shape[1]
                m, ks_, vs_ = swa_attention(h, past_k, past_v, past_valid, keep,
                                            attn_w_q[j], attn_w_k[j], attn_w_v[j], attn_w_o[j],
                                            attn_sinks[j], rel_bias)
                k_new[grp].append(ks_)
                v_new[grp].append(vs_)
            x = x + ga1[:, None, :] * rmsnorm(m, g_norm[i, 1])
            h = modulate(rmsnorm(x, g_norm[i, 2]), sh2, sc2)
            past_f = jnp.zeros((Bg, FFN_K - 1, D_FF), x.dtype) if grp == 0 else cache_ffn[i]
            f, fst = conv_glu_ffn(h, past_f, ffn_w_up[i], ffn_w_dw[i], ffn_b_dw[i], ffn_w_down[i])
            ffn_new[grp].append(fst)
            xs[grp] = x + ga2[:, None, :] * rmsnorm(f, g_norm[i, 3])
    conv_state_prompt = jnp.stack(conv_new[0])
    conv_state_sample = jnp.stack(conv_new[1])
    k_state_prompt = jnp.stack(k_new[0])
    v_state_prompt = jnp.stack(v_new[0])
    k_state_sample = jnp.stack(k_new[1])
    v_state_sample = jnp.stack(v_new[1])
    ffn_state_prompt = jnp.stack(ffn_new[0])
    ffn_state_sample = jnp.stack(ffn_new[1])
    return (xs[0], xs[1], conv_state_prompt, conv_state_sample, k_state_prompt, v_state_prompt,
            k_state_sample, v_state_sample, ffn_state_prompt, ffn_state_sample)
```

```python
from contextlib import ExitStack
import numpy as np
import concourse.bass as bass
import concourse.mybir as mybir
from concourse.bass_utils import run_bass_kernel_spmd

F32 = mybir.dt.float32
BF16 = mybir.dt.bfloat16
AF = mybir.ActivationFunctionType
ALU = mybir.AluOpType

D = 2048
NCH = 16
DFF = 5632
NJ = 44
T = 1312
TP = 1248
HALO = 224
NREAL = 1024
NOUT = 1088
EPS = 1e-6
NCORES = 8
SEQR = [(0, 1248, 0), (1248, 1280, 1), (1280, 1312, 2)]


class Buf:
    __slots__ = ("name", "last_w", "readers", "excl")

    def __init__(self, name, excl=False):
        self.name = name
        self.last_w = None
        self.readers = []
        self.excl = excl


class DmaSem:
    def __init__(self, sem):
        self.sem = sem
        self.count = 0


class Eng:
    def __init__(self, name, sem):
        self.name = name
        self.sem = sem
        self.count = 0
        self.ops = []
        self.waited = {}
        self.pending = False


class Sched:
    def __init__(self, nc, ctx):
        self.nc = nc
        self.ctx = ctx
        self.engs = {}
        for n in ("pe", "act", "dve", "pool", "sp"):
            self.engs[n] = Eng(n, ctx.enter_context(nc.semaphore("s_" + n)))
        self.dsems = []

    def dma_sem(self, name):
        d = DmaSem(self.ctx.enter_context(self.nc.semaphore(name)))
        self.dsems.append(d)
        return d

    def _deps(self, reads, writes):
        deps = []
        for b in reads:
            if b.last_w is not None:
                deps.append(b.last_w)
            if b.excl:
                deps.extend(b.readers)
        for b in writes:
            if b.last_w is not None:
                deps.append(b.last_w)
            deps.extend(b.readers)
        return deps

    def _waits(self, e, deps, skip_self=False):
        need = {}
        for (kind, obj, val) in deps:
            if kind == "eng" and skip_self and obj is e:
                continue
            sem = obj.sem
            k = id(sem)
            if e.waited.get(k, 0) >= val:
                continue
            if k not in need or need[k][1] < val:
                need[k] = (sem, val)
        out = []
        for k, (sem, val) in need.items():
            e.waited[k] = val
            out.append((sem, val))
        return out

    def op(self, eng, fn, reads=(), writes=(), inc=True, skip_self=False):
        e = self.engs[eng]
        waits = self._waits(e, self._deps(reads, writes), skip_self=skip_self)
        e.ops.append((waits, fn, ("eng", inc)))
        val = e.count + 1
        if inc:
            e.count += 1
            e.pending = False
        else:
            e.pending = True
        tag = ("eng", e, val)
        for b in writes:
            b.last_w = tag
            b.readers = []
        for b in reads:
            b.readers.append(tag)

    def dma(self, eng, fn, dsem, reads=(), writes=()):
        e = self.engs[eng]
        waits = self._waits(e, self._deps(reads, writes))
        e.ops.append((waits, fn, ("dma", dsem)))
        dsem.count += 16
        tag = ("dma", dsem, dsem.count)
        for b in writes:
            b.last_w = tag
            b.readers = []
        for b in reads:
            b.readers.append(tag)

    def finalize(self, bufs, dsem):
        tag = ("dma", dsem, dsem.count)
        for b in bufs:
            b.last_w = tag

    def barrier(self, names=("pe", "act", "dve", "pool", "sp"), dsems=None):
        for n in names:
            assert not self.engs[n].pending
        targets = [(self.engs[n].sem, self.engs[n].count) for n in names if self.engs[n].count > 0]
        for d in (self.dsems if dsems is None else dsems):
            if d.count > 0:
                targets.append((d.sem, d.count))
        for n in names:
            e = self.engs[n]
            waits = []
            for sem, val in targets:
                if sem is e.sem:
                    continue
                k = id(sem)
                if e.waited.get(k, 0) >= val:
                    continue
                e.waited[k] = val
                waits.append((sem, val))
            if waits:
                e.ops.append((waits, None, None))

    def wait_all(self, eng, dsems):
        e = self.engs[eng]
        waits = [(d.sem, d.count) for d in dsems if d.count > 0]
        e.ops.append((waits, None, None))

    def emit(self):
        nc = self.nc
        for n, e in self.engs.items():
            assert not e.pending, f"engine {n} has trailing non-inc ops"
        with nc.Block() as block:
            def run(e, h):
                for waits, fn, kind in e.ops:
                    for sem, val in waits:
                        h.wait_ge(sem, val)
                    if fn is None:
                        continue
                    ins = fn(h)
                    if kind[0] == "eng":
                        if kind[1]:
                            ins.then_inc(e.sem, 1)
                    else:
                        ins.then_inc(kind[1].sem, 16)

            @block.tensor
            def _(h):
                run(self.engs["pe"], h)

            @block.scalar
            def _(h):
                run(self.engs["act"], h)

            @block.vector
            def _(h):
                run(self.engs["dve"], h)

            @block.gpsimd
            def _(h):
                run(self.engs["pool"], h)

            @block.sync
            def _(h):
                run(self.engs["sp"], h)


def _pv_layout():
    off = {}
    n = 0
    for name, cols in [("gnorm", 8 * 16), ("bmod", 2 * 96), ("cbin", 32), ("cwdw", 31 * 16), ("cbdw", 16),
                       ("lng", 16), ("lnb", 16), ("cbout", 16), ("fwdw", 2 * 3 * NJ), ("fbdw", 2 * NJ),
                       ("cT", 48), ("hm", 1), ("sink", 16), ("ind", 34)]:
        off[name] = n
        n += cols
    return off, n


PVO, NV = _pv_layout()


def _t5_bucket(rel):
    n_buckets, max_distance = 32, 128
    half = n_buckets // 2
    max_exact = half // 2
    n = np.abs(rel)
    ret = np.where(rel > 0, half, 0)
    nf = np.maximum(n, 1).astype(np.float32)
    large = max_exact + (np.log(nf / max_exact) / np.float32(np.log(max_distance / max_exact))
                         * (half - max_exact)).astype(np.int32)
    large = np.minimum(large, half - 1)
    return ret + np.where(n < max_exact, n, large)


def _tiles(a, b, w=512):
    out = []
    while a < b:
        e = min(a + w, b)
        out.append((a, e))
        a = e
    return out


def _seq_pieces(a, b):
    out = []
    for (s0, s1, s) in SEQR:
        lo, hi = max(a, s0), min(b, s1)
        if lo < hi:
            out.append((lo, hi, s))
    return out


class Prog:
    def __init__(self):
        self.nc = bass.Bass("TRN2", target_bir_lowering=False)
        self.tasks = []

    def dram(self):
        nc = self.nc
        I = lambda n, s: nc.dram_tensor(n, s, F32, kind="ExternalInput").ap()
        O = lambda n, s: nc.dram_tensor(n, s, F32, kind="ExternalOutput").ap()
        self.xin = I("xin", [128, NCH, T])
        self.pvec = I("pvec", [128, NV])
        self.cconv = I("cconv", [128, NCH, 2, 30])
        self.ck = I("ck", [128, 4, 2, 128])
        self.cv = I("cv", [128, 2, 256])
        self.cffn = I("cffn", [128, 2, NJ, 2, 2])
        self.relb = I("relb", [32, 32])
        self.ebase = I("ebase", [32, 255])
        self.ident = I("ident", [128, 128])
        self.w_mod = I("w_mod", [2, D, 6 * D])
        self.w_in = I("w_in", [D, 2 * D])
        self.w_out = I("w_out", [D, D])
        self.w_q = I("w_q", [D, D])
        self.w_k = I("w_k", [D, 512])
        self.w_v = I("w_v", [D, 256])
        self.w_o = I("w_o", [D, D])
        self.w_up = I("w_up", [2, D, 2 * DFF])
        self.w_down = I("w_down", [2, DFF, D])
        self.o_y = O("o_y", [128, NCH, NOUT])
        self.o_ust = O("o_ust", [128, NCH, 90])
        self.o_kst = O("o_kst", [64, 4, 128])
        self.o_vst = O("o_vst", [128, 256])
        self.o_kss = O("o_kss", [64, 4, 2, 128])
        self.o_vss = O("o_vss", [2, 128, 256])
        self.o_fst = O("o_fst", [128, 2, NJ, 6])
        self.rs = nc.dram_tensor("rs", [128, NCH, T], F32, kind="Internal").ap()

    def alloc(self, ctx):
        nc = self.nc
        S = self.S
        sb = lambda n, s, dt: ctx.enter_context(nc.sbuf_tensor(n, s, dt))
        self.H = sb("H", [128, NCH, T], BF16)
        self.Yt = sb("Y", [128, NCH * T], BF16)
        self.Y = self.Yt[:, :].rearrange("p (c t) -> p c t", c=NCH)
        self.RWt = sb("RW", [128, NCH * T], F32)
        self.RW = self.RWt[:, :].rearrange("p (c t) -> p c t", c=NCH)
        self.WT = [sb(f"WT{i}", [128, 4096], BF16) for i in range(3)]
        self.PV = sb("PV", [128, NV], F32)
        self.MODT = sb("MODT", [128, 2, 96, 3], F32)
        self.DER = sb("DER", [128, 2, 4, NCH, 3], F32)
        self.ONES = sb("ONES", [128, 128], BF16)
        self.SCB = sb("SCB", [128, NCH, 3], BF16)
        self.ST1 = sb("ST1", [128, T], F32)
        self.SQ = [sb(f"SQ{i}", [128, 512], BF16) for i in range(2)]
        self.CFFN = sb("CFFN", [128, 2, NJ, 2, 2], F32)
        self.KMB = sb("KMB", [128, 36], F32)
        self.PS = [ctx.enter_context(nc.psum_tensor(f"ps{i}", [128, 512], F32)) for i in range(8)]
        self.bH = [[Buf(f"H{c}_{t}") for t in range(3)] for c in range(NCH)]
        self.bY = [[Buf(f"Y{c}_{t}") for t in range(3)] for c in range(NCH)]
        self.bRW = [[Buf(f"RW{c}_{t}") for t in range(3)] for c in range(NCH)]
        self.bW = [Buf("W0"), Buf("W1"), Buf("W2")]
        self.dW = [S.dma_sem("dW0"), S.dma_sem("dW1"), S.dma_sem("dW2")]
        self.bPS = [Buf(f"PS{i}", excl=True) for i in range(8)]
        self.bPV = Buf("PV")
        self.bMODT = Buf("MODT")
        self.bDER = Buf("DER")
        self.bONES = Buf("ONES")
        self.bSCB = Buf("SCB")
        self.bST1 = [Buf(f"ST1_{t}") for t in range(3)]
        self.bST2 = [Buf(f"ST2_{t}") for t in range(3)]
        self.bSQ = [Buf("SQ0"), Buf("SQ1")]
        self.bCFFN = Buf("CFFN")
        self.bKMB = Buf("KMB")
        self.d_in = S.dma_sem("d_in")
        self.d_out = S.dma_sem("d_out")
        self.d_rs = S.dma_sem("d_rs")
        self.d_xs = [S.dma_sem("d_xs0"), S.dma_sem("d_xs1")]
        self.sq_i = 0
        self.rot = list(range(8))
        self.rot_i = 0
        self.wslot = 0

    @staticmethod
    def tix(a, b):
        return list(range(a // 512, (b - 1) // 512 + 1))

    def bufs(self, table, c, a, b):
        return [table[c][t] for t in self.tix(a, b)]

    def pin(self, idxs):
        self.rot = [i for i in self.rot if i not in idxs]
        self.rot_i = 0

    def unpin(self, idxs):
        self.rot = sorted(set(self.rot) | set(idxs))
        self.rot_i = 0

    def bank(self):
        i = self.rot[self.rot_i % len(self.rot)]
        self.rot_i += 1
        return i

    def mm(self, bi, out_ap, pairs, reads, first=True, last=True):
        n = len(pairs)
        S = self.S
        for i, (l, r) in enumerate(pairs):
            S.op("pe",
                 lambda h, l=l, r=r, st=(first and i == 0), sp=(last and i == n - 1): h.matmul(out_ap, lhsT=l, rhs=r, start=st, stop=sp),
                 reads=reads if i == 0 else (), writes=[self.bPS[bi]] if i == 0 else (),
                 inc=(i == n - 1), skip_self=True)

    def act(self, out, in_, func, reads, writes, scale=1.0, bias=0.0):
        self.S.op("act", lambda h: h.activation(out=out, in_=in_, func=func, scale=scale, bias=bias),
                  reads=reads, writes=writes)

    def tt(self, eng, out, in0, in1, op, reads, writes):
        self.S.op(eng, lambda h: h.tensor_tensor(out=out, in0=in0, in1=in1, op=op), reads=reads, writes=writes)

    def ts(self, eng, out, in0, s1, s2, op0, op1, reads, writes):
        if s2 is None:
            self.S.op(eng, lambda h: h.tensor_scalar(out=out, in0=in0, scalar1=s1, scalar2=None, op0=op0),
                      reads=reads, writes=writes)
        else:
            self.S.op(eng, lambda h: h.tensor_scalar(out=out, in0=in0, scalar1=s1, scalar2=s2, op0=op0, op1=op1),
                      reads=reads, writes=writes)

    def stt(self, out, in0, scalar, in1, op0, op1, reads, writes):
        self.S.op("dve", lambda h: h.scalar_tensor_tensor(out=out, in0=in0, scalar=scalar, in1=in1, op0=op0, op1=op1),
                  reads=reads, writes=writes)

    def cp(self, eng, out, in_, reads, writes):
        self.S.op(eng, lambda h: h.tensor_copy(out=out, in_=in_), reads=reads, writes=writes)

    def pvc(self, name, col):
        o = PVO[name] + col
        return self.PV[:, o:o + 1]

    def task(self, load, run):
        self.tasks.append((load, run))

    def wload(self, parts):
        s = self.wslot
        self.wslot = (self.wslot + 1) % 3
        for (dst_off, ncol, nchk, src) in parts:
            dst = self.WT[s][:, dst_off:dst_off + nchk * ncol].rearrange("p (c f) -> p c f", c=nchk)
            srcv = src.rearrange("(c p) f -> p c f", p=128)
            self.S.dma("pool", lambda h, dst=dst, srcv=srcv: h.dma_start(out=dst, in_=srcv), self.dW[s],
                       writes=[self.bW[s]])
        return s

    def run_tasks(self):
        tasks = self.tasks
        slots = {}
        loads = [i for i, (ld, _) in enumerate(tasks) if ld is not None]
        li = 0
        for i, (ld, run) in enumerate(tasks):
            if ld is not None:
                while li < len(loads) and loads[li] <= i:
                    k = loads[li]
                    slots[k] = tasks[k][0]()
                    li += 1
                ahead = 0
                for k2 in loads[li:li + 2]:
                    ahead += 1
                while li < len(loads) and sum(1 for k2 in loads[:li] if k2 > i) < 2:
                    k = loads[li]
                    slots[k] = tasks[k][0]()
                    li += 1
            run(slots.get(i))

    def sq_buf(self):
        i = self.sq_i
        self.sq_i ^= 1
        return i

    def stat_mm(self, bank_i, ncols, rhs, reads, first, last):
        self.mm(bank_i, self.PS[bank_i][:, 0:ncols], [(self.ONES[:, :], rhs)], reads + [self.bONES], first=first, last=last)

    def rstd_from(self, tiles, banks, st, bst):
        for ti, (a, b) in enumerate(tiles):
            w = b - a
            for t in self.tix(a, b):
                pass
            bw = [bst[t] for t in self.tix(a, b)]
            self.act(st[:, a:b], self.PS[banks[ti]][:, 0:w], AF.Sqrt, [self.bPS[banks[ti]]], bw, scale=1.0 / D, bias=EPS)
            self.S.op("dve", lambda h, a=a, b=b: h.reciprocal(out=st[:, a:b], in_=st[:, a:b]), reads=bw, writes=bw)

    def x_stats(self, tiles):
        banks = [5, 6, 7]
        self.pin(banks)
        for c in range(NCH):
            for ti, (a, b) in enumerate(tiles):
                q = self.sq_buf()
                w = b - a
                self.act(self.SQ[q][:, 0:w], self.RW[:, c, a:b], AF.Square, self.bufs(self.bRW, c, a, b), [self.bSQ[q]])
                self.stat_mm(banks[ti], w, self.SQ[q][:, 0:w], [self.bSQ[q]], first=(c == 0), last=(c == NCH - 1))
        self.rstd_from(tiles, banks, self.ST1, self.bST1)
        self.unpin(banks)

    def norm_apply(self, tiles, layer, ka, kb_which):
        TN = self.TN
        for c in range(NCH):
            for ti, (a, b) in enumerate(tiles):
                k = (c * 3 + ti) % 2
                w = b - a
                self.tt("dve", TN[k][:, 0:w], self.RW[:, c, a:b], self.ST1[:, a:b], ALU.mult,
                        self.bufs(self.bRW, c, a, b) + self.bufs_st(self.bST1, a, b), [self.bTN[k]])
                for (lo, hi, s) in _seq_pieces(a, b):
                    self.act(self.H[:, c, lo:hi], TN[k][:, lo - a:hi - a], AF.Identity,
                             [self.bTN[k], self.bDER, self.bMODT], self.bufs(self.bH, c, lo, hi),
                             scale=self.DER[:, layer, ka, c, s:s + 1],
                             bias=self.MODT[:, layer, kb_which * 16 + c, s:s + 1])

    def bufs_st(self, table, a, b):
        return [table[t] for t in self.tix(a, b)]

    def mod_tasks(self, layer, blk_lo, blk_hi, modbank):
        def mk(blk):
            def load():
                return self.wload([(0, 256, NCH, self.w_mod[layer, :, blk * 256:(blk + 1) * 256])])

            def run(s):
                wt = self.WT[s][:, :].rearrange("p (c f) -> p c f", c=NCH)
                for jj in range(2):
                    jc = blk * 2 + jj
                    pairs = [(wt[:, c, jj * 128:(jj + 1) * 128], self.SCB[:, c, :]) for c in range(NCH)]
                    self.mm(modbank, self.PS[modbank][:, jc * 3:jc * 3 + 3], pairs, [self.bW[s], self.bSCB])
            return load, run
        for blk in range(blk_lo, blk_hi):
            ld, rn = mk(blk)
            self.task(ld, rn)

    def mod_evac(self, layer, jlo, jhi, modbank):
        o = PVO["bmod"] + layer * 96
        n = jhi - jlo
        self.tt("dve", self.MODT[:, layer, jlo:jhi, :],
                self.PS[modbank][:, jlo * 3:jhi * 3].rearrange("p (j s) -> p j s", s=3),
                self.PV[:, o + jlo:o + jhi].unsqueeze(2).broadcast_to([128, n, 3]), ALU.add,
                [self.bPS[modbank], self.bPV], [self.bMODT])

    def mod_derive(self, layer, which_scale, gidx, kder, is_gate):
        g = self.PV[:, PVO["gnorm"] + (layer * 4 + gidx) * 16: PVO["gnorm"] + (layer * 4 + gidx + 1) * 16]
        gb = g.unsqueeze(2).broadcast_to([128, NCH, 3])
        src = self.MODT[:, layer, which_scale * 16:(which_scale + 1) * 16, :]
        dst = self.DER[:, layer, kder, :, :]
        if is_gate:
            self.tt("dve", dst, src, gb, ALU.mult, [self.bMODT, self.bPV], [self.bDER])
        else:
            self.stt(dst, src, 1.0, gb, ALU.add, ALU.mult, [self.bMODT, self.bPV], [self.bDER])

    def resid_update(self, a0, a1, layer, kg, src_dram, fsrc=None):
        XS = self.XS
        for c in range(NCH):
            k = c % 2
            self.S.dma("sp", lambda h, c=c, k=k: h.dma_start(out=XS[k][:, a0:a1], in_=src_dram[:, c, a0:a1]),
                       self.d_xs[k], writes=[self.bXS[k]])
            for (lo, hi, s) in _seq_pieces(a0, a1):
                g = self.DER[:, layer, kg, c, s:s + 1]
                if fsrc is None:
                    self.stt(self.RW[:, c, lo:hi], self.RW[:, c, lo:hi], g, self.ST1[:, lo:hi], ALU.mult, ALU.mult,
                             self.bufs(self.bRW, c, lo, hi) + self.bufs_st(self.bST1, lo, hi) + [self.bDER],
                             self.bufs(self.bRW, c, lo, hi))
                else:
                    self.stt(self.RW[:, c, lo:hi], fsrc[:, c, lo:hi], g, self.ST1[:, lo:hi], ALU.mult, ALU.mult,
                             self.bufs(self.bH, c, lo, hi) + self.bufs_st(self.bST1, lo, hi) + [self.bDER],
                             self.bufs(self.bRW, c, lo, hi))
            self.tt("dve", self.RW[:, c, a0:a1], self.RW[:, c, a0:a1], XS[k][:, a0:a1], ALU.add,
                    self.bufs(self.bRW, c, a0, a1) + [self.bXS[k]], self.bufs(self.bRW, c, a0, a1))

    def proj_stats_tasks(self, w_ap, tiles, bias_name, src, bsrc):
        banks = [5, 6, 7]

        def mk(blk):
            def load():
                return self.wload([(0, 256, NCH, w_ap[:, blk * 256:(blk + 1) * 256])])

            def run(s):
                if blk == 0:
                    self.pin(banks)
                wt = self.WT[s][:, :].rearrange("p (c f) -> p c f", c=NCH)
                for ii in range(2):
                    i = blk * 2 + ii
                    for ti, (a, b) in enumerate(tiles):
                        w = b - a
                        bi = self.bank()
                        pairs = [(wt[:, c, ii * 128:(ii + 1) * 128], src[:, c, a:b]) for c in range(NCH)]
                        rd = [self.bW[s]]
                        for c in range(NCH):
                            rd += self.bufs(bsrc, c, a, b)
                        self.mm(bi, self.PS[bi][:, 0:w], pairs, rd)
                        bias = self.pvc(bias_name, i) if bias_name else 0.0
                        rdb = [self.bPS[bi]] + ([self.bPV] if bias_name else [])
                        self.act(self.RW[:, i, a:b], self.PS[bi][:, 0:w], AF.Identity, rdb, self.bufs(self.bRW, i, a, b), bias=bias)
                        q = self.sq_buf()
                        self.act(self.SQ[q][:, 0:w], self.PS[bi][:, 0:w], AF.Square, rdb, [self.bSQ[q]], bias=bias)
                        self.stat_mm(banks[ti], w, self.SQ[q][:, 0:w], [self.bSQ[q]], first=(i == 0), last=(i == NCH - 1))
                if blk == 7:
                    self.rstd_from(tiles, banks, self.ST1, self.bST1)
                    self.unpin(banks)
            return load, run
        for blk in range(8):
            ld, rn = mk(blk)
            self.task(ld, rn)

    def ffn_tasks(self, layer, base, mod_next=None):
        S = self.S
        W = T - base
        tiles = _tiles(base, T)
        npc = TP - base
        GW = 2 + npc + 68

        def gidx(col):
            if col < TP:
                return 2 + col - base
            s = (col - TP) // 32
            return 2 + npc + s * 34 + 2 + (col - TP - s * 32)

        def mk_up(j):
            def load():
                return self.wload([(0, 256, NCH, self.w_up[layer, :, j * 256:(j + 1) * 256])])

            def run(s):
                HID = self.HID
                GEXT, CVb = self.GEXT, self.CVb
                bG, bC = self.bGEXT, self.bCVb
                bHID = self.bHID
                wt = self.WT[s][:, :].rearrange("p (c f) -> p c f", c=NCH)
                if j == 0:
                    S.op("pool", lambda h: h.memset(GEXT[:, 0:2], 0.0), writes=[bG])
                self.cp("pool", GEXT[:, 2 + npc:2 + npc + 68].rearrange("p (s t) -> p s t", s=2)[:, :, 0:2],
                        self.CFFN[:, layer, j, :, :], [self.bCFFN], [bG])
                vb = []
                for ti, (a, b) in enumerate(tiles):
                    w = b - a
                    rd = [self.bW[s]]
                    for c in range(NCH):
                        rd += self.bufs(self.bH, c, a, b)
                    bg = self.bank()
                    self.mm(bg, self.PS[bg][:, 0:w], [(wt[:, c, 0:128], self.H[:, c, a:b]) for c in range(NCH)], rd)
                    bv = self.bank()
                    self.mm(bv, self.PS[bv][:, 0:w], [(wt[:, c, 128:256], self.H[:, c, a:b]) for c in range(NCH)], rd)
                    vb.append(bv)
                    pa, pb = a, min(b, TP)
                    if pa < pb:
                        self.act(GEXT[:, gidx(pa):gidx(pa) + (pb - pa)], self.PS[bg][:, pa - a:pb - a], AF.Copy, [self.bPS[bg]], [bG])
                    if b > TP:
                        sa = max(a, TP)
                        assert sa == TP and b == T
                        self.act(GEXT[:, 2 + npc:2 + npc + 68].rearrange("p (s t) -> p s t", s=2)[:, :, 2:34],
                                 self.PS[bg][:, TP - a:T - a].rearrange("p (s t) -> p s t", s=2), AF.Copy, [self.bPS[bg]], [bG])
                self.ts("dve", GEXT[:, gidx(222):gidx(222) + 2], GEXT[:, gidx(222):gidx(222) + 2], self.pvc("hm", 0), None,
                        ALU.mult, None, [bG, self.bPV], [bG])
                self.cp("pool", self.FSTATE[:, j, 0:2], GEXT[:, gidx(1246):gidx(1246) + 2], [bG], [self.bFSTATE])
                self.cp("pool", self.FSTATE[:, j, 2:6].rearrange("p (s t) -> p s t", s=2),
                        GEXT[:, 2 + npc:2 + npc + 68].rearrange("p (s t) -> p s t", s=2)[:, :, 32:34], [bG], [self.bFSTATE])
                wo = PVO["fwdw"] + layer * 3 * NJ
                wk = [self.PV[:, wo + k * NJ + j: wo + k * NJ + j + 1] for k in range(3)]
                bo = self.PV[:, PVO["fbdw"] + layer * NJ + j: PVO["fbdw"] + layer * NJ + j + 1]
                for part in range(2):
                    if part == 0:
                        src = lambda k: GEXT[:, k:k + npc]
                        dst = CVb[:, 0:npc]
                    else:
                        src = lambda k: GEXT[:, 2 + npc:2 + npc + 68].rearrange("p (s t) -> p s t", s=2)[:, :, k:k + 32]
                        dst = CVb[:, npc:npc + 64].rearrange("p (s t) -> p s t", s=2)
                    self.ts("dve", dst, src(0), wk[0], bo, ALU.mult, ALU.add, [bG, self.bPV], [bC])
                    self.stt(dst, src(1), wk[1], dst, ALU.mult, ALU.add, [bG, bC, self.bPV], [bC])
                    self.stt(dst, src(2), wk[2], dst, ALU.mult, ALU.add, [bG, bC, self.bPV], [bC])
                self.act(CVb[:, 0:W], CVb[:, 0:W], AF.Gelu_apprx_tanh, [bC], [bC])
                for ti, (a, b) in enumerate(tiles):
                    self.tt("dve", HID(j)[:, a - base:b - base], CVb[:, a - base:b - base], self.PS[vb[ti]][:, 0:b - a], ALU.mult,
                            [bC, self.bPS[vb[ti]]], [bHID[j]])
                if j == NJ - 1:
                    S.dma("sp", lambda h, fs=self.FSTATE: h.dma_start(out=self.o_fst[:, layer, :, :], in_=fs[:, :, :]), self.d_out,
                          reads=[self.bFSTATE])
            return load, run

        stat_banks = [5, 6, 7]

        def mk_down(i, hh):
            def load():
                return self.wload([(0, 128, 22, self.w_down[layer, hh * 2816:(hh + 1) * 2816, i * 128:(i + 1) * 128])])

            def run(s):
                HID = self.HID
                bHID = self.bHID
                if i == 0 and hh == 0:
                    self.pin(stat_banks)
                wt = self.WT[s][:, 0:22 * 128].rearrange("p (c f) -> p c f", c=22)
                if hh == 0:
                    self.dbanks = [self.bank() for _ in tiles]
                for ti, (a, b) in enumerate(tiles):
                    w = b - a
                    bi = self.dbanks[ti]
                    pairs = [(wt[:, jj, :], HID(hh * 22 + jj)[:, a - base:b - base]) for jj in range(22)]
                    rd = [self.bW[s]] + [bHID[hh * 22 + jj] for jj in range(22)]
                    self.mm(bi, self.PS[bi][:, 0:w], pairs, rd, first=(hh == 0), last=(hh == 1))
                    if hh == 1:
                        self.act(self.H[:, i, a:b], self.PS[bi][:, 0:w], AF.Copy, [self.bPS[bi]], self.bufs(self.bH, i, a, b))
                        q = self.sq_buf()
                        self.act(self.SQ[q][:, 0:w], self.PS[bi][:, 0:w], AF.Square, [self.bPS[bi]], [self.bSQ[q]])
                        self.stat_mm(stat_banks[ti], w, self.SQ[q][:, 0:w], [self.bSQ[q]], first=(i == 0), last=(i == NCH - 1))
                if i == NCH - 1 and hh == 1:
                    self.rstd_from(tiles, stat_banks, self.ST1, self.bST1)
                    self.unpin(stat_banks)
            return load, run

        for j in range(NJ):
            ld, rn = mk_up(j)
            self.task(ld, rn)
            if mod_next is not None:
                mod_next(j)
        for i in range(NCH):
            for hh in range(2):
                ld, rn = mk_down(i, hh)
                self.task(ld, rn)

    def build(self):
        nc = self.nc
        self.dram()
        with ExitStack() as ctx:
            self.S = S = Sched(nc, ctx)
            self.alloc(ctx)
            self.program()
            self.run_tasks()
            S.barrier()
            S.wait_all("sp", S.dsems)
            S.emit()
        return nc

    def program(self):
        S = self.S
        nc = self.nc
        RWt, Yt = self.RWt, self.Yt
        Yf = Yt[:, :].bitcast(F32)
        ALLT = _tiles(0, T)
        MODB = 4

        self.TN = [Yf[:, 0:512], Yf[:, 512:1024]]
        self.bTN = [Buf("TN0"), Buf("TN1")]
        self.XS = [Yf[:, 1024:1024 + T], Yf[:, 1024 + T:1024 + 2 * T]]
        self.bXS = [Buf("XS0"), Buf("XS1")]

        def p0(_):
            S.dma("sp", lambda h: h.dma_start(out=self.PV[:, :], in_=self.pvec), self.d_in, writes=[self.bPV])
            for c in range(NCH):
                S.dma("sp", lambda h, c=c: h.dma_start(out=self.RW[:, c, :], in_=self.xin[:, c, :]), self.d_in,
                      writes=self.bRW[c])
            S.dma("sp", lambda h: h.dma_start(out=self.CFFN[:, :, :, :, :], in_=self.cffn), self.d_in, writes=[self.bCFFN])
            S.finalize([self.bPV, self.bCFFN] + [b for c in range(NCH) for b in self.bRW[c]], self.d_in)
            S.op("pool", lambda h: h.memset(self.ONES[:, :], 1.0), writes=[self.bONES])
            cT = self.PV[:, PVO["cT"]:PVO["cT"] + 48].rearrange("p (s c) -> p c s", s=3)
            self.act(self.SCB[:, :, :], cT, AF.Silu, [self.bPV], [self.bSCB])
            self.pin([MODB])
            self.x_stats(ALLT)
        self.task(None, p0)
        self.mod_tasks(0, 0, 16, MODB)

        def p0b(_):
            self.mod_evac(0, 0, 32, MODB)
            self.mod_derive(0, 1, 0, 0, False)
            self.norm_apply(ALLT, 0, 0, 0)
            S.barrier(("pe", "act", "dve", "pool", "sp"))
            self.conv_phaseA_setup()
        self.task(None, p0b)

        for c in range(NCH):
            self.conv_chunk_task(c)
            self.mod_tasks(0, 16 + 2 * c, 18 + 2 * c, MODB)

        def p1b(_):
            self.mod_evac(0, 32, 96, MODB)
            self.unpin([MODB])
            self.mod_derive(0, 2, 1, 1, True)
            self.mod_derive(0, 4, 2, 2, False)
            self.mod_derive(0, 5, 3, 3, True)
            self.conv_mm(NCH - 1)
            S.dma("sp", lambda h: h.dma_start(out=self.o_ust, in_=self.USTATE[:, :, :]), self.d_out, reads=[self.bUSTATE])
            self.conv_ln()
            S.barrier(("pe", "act", "dve", "pool", "sp"))
        self.task(None, p1b)

        self.proj_stats_tasks(self.w_out, ALLT, "cbout", self.Y, self.bY)

        def p2b(_):
            S.barrier(("pe", "act", "dve", "pool", "sp"))
            self.resid_update(0, T, 0, 1, self.xin)
            self.x_stats(ALLT)
            self.norm_apply(ALLT, 0, 2, 3)
            S.dma("sp", lambda h: h.dma_start(out=self.rs, in_=self.RW[:, :, :]), self.d_rs,
                  reads=[b for c in range(NCH) for b in self.bRW[c]])
            S.barrier(("pe", "act", "dve", "pool", "sp"))
            self.ffn_setup(30)
            self.pin([MODB])
        self.task(None, p2b)

        def modnext(j):
            lo = (48 * j) // NJ
            hi = (48 * (j + 1)) // NJ
            self.mod_tasks(1, lo, hi, MODB)
            if j == NJ - 1:
                def fin(_):
                    self.mod_evac(1, 0, 96, MODB)
                    self.unpin([MODB])
                    self.mod_derive(1, 1, 0, 0, False)
                    self.mod_derive(1, 2, 1, 1, True)
                    self.mod_derive(1, 4, 2, 2, False)
                    self.mod_derive(1, 5, 3, 3, True)
                self.task(None, fin)
        self.ffn_tasks(0, 30, modnext)

        def p4b(_):
            S.barrier(("pe", "act", "dve", "pool", "sp"))
            self.zero_rw_head(30)
            self.resid_update(30, T, 0, 3, self.rs, fsrc=self.H)
            self.x_stats(ALLT)
            self.norm_apply(ALLT, 1, 0, 0)
            S.dma("sp", lambda h: h.dma_start(out=self.rs, in_=self.RW[:, :, :]), self.d_rs,
                  reads=[b for c in range(NCH) for b in self.bRW[c]])
            S.barrier(("pe", "act", "dve", "pool", "sp"))
            self.attn_setup()
        self.task(None, p4b)

        self.attn_proj_tasks()

        def p7(_):
            self.attn_core()
            S.barrier(("pe", "act", "dve", "pool", "sp"))
        self.task(None, p7)

        L1T = _tiles(160, T)
        self.proj_stats_tasks(self.w_o, L1T, None, self.H, self.bH)

        def p8b(_):
            S.barrier(("pe", "act", "dve", "pool", "sp"))
            self.zero_rw_head(160)
            self.resid_update(160, T, 1, 1, self.rs)
            self.x_stats(ALLT)
            self.norm_apply(ALLT, 1, 2, 3)
            S.dma("sp", lambda h: h.dma_start(out=self.rs, in_=self.RW[:, :, :]), self.d_rs,
                  reads=[b for c in range(NCH) for b in self.bRW[c]])
            S.barrier(("pe", "act", "dve", "pool", "sp"))
            self.ffn_setup(160)
        self.task(None, p8b)

        self.ffn_tasks(1, 160, None)

        def p11(_):
            S.barrier(("pe", "act", "dve", "pool", "sp"))
            self.resid_update(160, T, 1, 3, self.rs, fsrc=self.H)
            S.dma("sp", lambda h: h.dma_start(out=self.o_y, in_=self.RW[:, :, HALO:T]), self.d_out,
                  reads=[b for c in range(NCH) for b in self.bRW[c]])
        self.task(None, p11)

    def zero_rw_head(self, ncols):
        for c in range(NCH):
            self.S.op("dve", lambda h, c=c: h.memset(self.RW[:, c, 0:ncols], 0.0), writes=[self.bRW[c][0]])

    def conv_phaseA_setup(self):
        RWt = self.RWt
        RWb = RWt[:, :].bitcast(BF16)
        o = 0
        UW = 30 + TP + 2 * 62
        self.UW = UW
        self.UB = []
        for i in range(2):
            self.UB.append(RWb[:, 2 * o:2 * o + UW]); o += (UW + 1) // 2
        self.DG = []
        for i in range(2):
            self.DG.append(RWb[:, 2 * o:2 * o + 31 * 128].rearrange("p (k f) -> p k f", k=31)); o += 31 * 64
        self.ACC = []
        for i in range(2):
            self.ACC.append(RWt[:, o:o + T]); o += T
        self.SIG = []
        for i in range(2):
            self.SIG.append(RWt[:, o:o + 512]); o += 512
        self.USTATE = RWt[:, o:o + NCH * 90].rearrange("p (c t) -> p c t", c=NCH); o += NCH * 90
        self.CCONV = RWt[:, o:o + NCH * 60].rearrange("p (c s t) -> p c s t", c=NCH, s=2); o += NCH * 60
        self.IDF = RWt[:, o:o + 128]; o += 128
        self.IDB = RWb[:, 2 * o:2 * o + 128]; o += 64
        self.ST2 = RWt[:, o:o + T]; o += T
        assert o <= NCH * T
        self.bUB = [Buf("UB0"), Buf("UB1")]
        self.bDG = [Buf("DG0"), Buf("DG1")]
        self.bACC = [Buf("ACC0"), Buf("ACC1")]
        self.bSIG = [Buf("SIG0"), Buf("SIG1")]
        self.bUSTATE = Buf("USTATE")
        self.bCCONV = Buf("CCONV")
        self.bIDF = Buf("IDF")
        self.bIDB = Buf("IDB")
        self.sig_i = 0
        S = self.S
        S.dma("sp", lambda h: h.dma_start(out=self.CCONV[:, :, :, :], in_=self.cconv), self.d_in, writes=[self.bCCONV])
        S.dma("sp", lambda h: h.dma_start(out=self.IDF[:, :], in_=self.ident), self.d_in, writes=[self.bIDF])
        S.finalize([self.bCCONV, self.bIDF], self.d_in)
        self.cp("pool", self.IDB[:, :], self.IDF[:, :], [self.bIDF], [self.bIDB])
        for i in range(2):
            S.op("pool", lambda h, i=i: h.memset(self.UB[i][:, 0:30], 0.0), writes=[self.bUB[i]])

    def conv_mm(self, c):
        ub = c % 2
        U, bU = self.UB[ub], self.bUB[ub]
        DG, bD = self.DG[ub], self.bDG[ub]
        samp = U[:, 30 + TP:30 + TP + 124].rearrange("p (s t) -> p s t", s=2)
        for (a, b) in [(0, 512), (512, 1024), (1024, TP)]:
            bi = self.bank()
            self.mm(bi, self.PS[bi][:, 0:b - a], [(DG[:, k, :], U[:, a + k:b + k]) for k in range(31)], [bU, bD])
            self.act(self.Y[:, c, a:b], self.PS[bi][:, 0:b - a], AF.Identity, [self.bPS[bi], self.bPV], self.bufs(self.bY, c, a, b),
                     bias=self.pvc("cbdw", c))
        bi = self.bank()
        self.mm(bi, self.PS[bi][:, 0:64].rearrange("p (s t) -> p s t", s=2), [(DG[:, k, :], samp[:, :, k:k + 32]) for k in range(31)], [bU, bD])
        self.act(self.Y[:, c, TP:T], self.PS[bi][:, 0:64], AF.Identity, [self.bPS[bi], self.bPV], self.bufs(self.bY, c, TP, T),
                 bias=self.pvc("cbdw", c))

    def conv_chunk_task(self, c):
        S = self.S
        ALLT = _tiles(0, T)

        def load():
            return self.wload([(0, 256, NCH, self.w_in[:, c * 256:(c + 1) * 256])])

        def run(s):
            wt = self.WT[s][:, :].rearrange("p (c f) -> p c f", c=NCH)
            ub = c % 2
            U, bU = self.UB[ub], self.bUB[ub]
            samp = U[:, 30 + TP:30 + TP + 124].rearrange("p (s t) -> p s t", s=2)
            self.cp("pool", samp[:, :, 0:30], self.CCONV[:, c, :, :], [self.bCCONV], [bU])
            wo = PVO["cwdw"]
            for k in range(31):
                self.ts("dve", self.DG[ub][:, k, :], self.IDB[:, :], self.PV[:, wo + k * 16 + c:wo + k * 16 + c + 1], None,
                        ALU.mult, None, [self.bIDB, self.bPV], [self.bDG[ub]])
            for ti, (a, b) in enumerate(ALLT):
                w = b - a
                rd = [self.bW[s]]
                for k in range(NCH):
                    rd += self.bufs(self.bH, k, a, b)
                bg = self.bank()
                self.mm(bg, self.PS[bg][:, 0:w], [(wt[:, k, 128:256], self.H[:, k, a:b]) for k in range(NCH)], rd)
                ba = self.bank()
                self.mm(ba, self.PS[ba][:, 0:w], [(wt[:, k, 0:128], self.H[:, k, a:b]) for k in range(NCH)], rd)
                si = self.sig_i
                self.sig_i ^= 1
                self.act(self.SIG[si][:, 0:w], self.PS[bg][:, 0:w], AF.Sigmoid, [self.bPS[bg], self.bPV], [self.bSIG[si]],
                         bias=self.pvc("cbin", 16 + c))
                pb = min(b, TP)
                rdu = [self.bPS[ba], self.bSIG[si], self.bPV]
                if a < pb:
                    self.stt(U[:, 30 + a:30 + pb], self.PS[ba][:, 0:pb - a], self.pvc("cbin", c), self.SIG[si][:, 0:pb - a],
                             ALU.add, ALU.mult, rdu, [bU])
                if b > TP:
                    assert b == T and a <= 1218
                    ps3 = self.PS[ba][:, TP - a:T - a].rearrange("p (s t) -> p s t", s=2)
                    sg3 = self.SIG[si][:, TP - a:T - a].rearrange("p (s t) -> p s t", s=2)
                    self.stt(samp[:, :, 30:62], ps3, self.pvc("cbin", c), sg3, ALU.add, ALU.mult, rdu, [bU])
                    self.stt(self.USTATE[:, c, 0:30], self.PS[ba][:, 1218 - a:TP - a], self.pvc("cbin", c),
                             self.SIG[si][:, 1218 - a:TP - a], ALU.add, ALU.mult, rdu, [self.bUSTATE])
                    self.stt(self.USTATE[:, c, 30:90].rearrange("p (s t) -> p s t", s=2), ps3[:, :, 2:32], self.pvc("cbin", c),
                             sg3[:, :, 2:32], ALU.add, ALU.mult, rdu, [self.bUSTATE])
            self.ts("dve", U[:, 30 + 194:30 + 224], U[:, 30 + 194:30 + 224], self.pvc("hm", 0), None, ALU.mult, None,
                    [bU, self.bPV], [bU])
            if c > 0:
                self.conv_mm(c - 1)
        self.task(load, run)

    def conv_ln(self):
        S = self.S
        ALLT = _tiles(0, T)
        mb = [0, 1, 2]
        qb = [5, 6, 7]
        self.pin(mb + qb)
        for c in range(NCH):
            for ti, (a, b) in enumerate(ALLT):
                w = b - a
                q = self.sq_buf()
                self.act(self.SQ[q][:, 0:w], self.Y[:, c, a:b], AF.Square, self.bufs(self.bY, c, a, b), [self.bSQ[q]])
                self.stat_mm(mb[ti], w, self.Y[:, c, a:b], self.bufs(self.bY, c, a, b), first=(c == 0), last=(c == NCH - 1))
                self.stat_mm(qb[ti], w, self.SQ[q][:, 0:w], [self.bSQ[q]], first=(c == 0), last=(c == NCH - 1))
        for ti, (a, b) in enumerate(ALLT):
            w = b - a
            b1, b2 = [self.bST1[ti]], [self.bST2[ti]]
            self.act(self.ST2[:, a:b], self.PS[mb[ti]][:, 0:w], AF.Copy, [self.bPS[mb[ti]]], b2, scale=1.0 / D)
            self.tt("dve", self.ST1[:, a:b], self.ST2[:, a:b], self.ST2[:, a:b], ALU.mult, b2, b1)
            self.stt(self.ST1[:, a:b], self.PS[qb[ti]][:, 0:w], 1.0 / D, self.ST1[:, a:b], ALU.mult, ALU.subtract,
                     [self.bPS[qb[ti]]] + b1, b1)
            self.act(self.ST1[:, a:b], self.ST1[:, a:b], AF.Sqrt, b1, b1, bias=EPS)
            S.op("dve", lambda h, a=a, b=b: h.reciprocal(out=self.ST1[:, a:b], in_=self.ST1[:, a:b]), reads=b1, writes=b1)
        self.unpin(mb + qb)
        for c in range(NCH):
            k = c % 2
            A, bA = self.ACC[k], self.bACC[k]
            self.tt("dve", A[:, 0:T], self.Y[:, c, :], self.ST2[:, :], ALU.subtract, self.bY[c] + self.bST2, [bA])
            self.tt("dve", A[:, 0:T], A[:, 0:T], self.ST1[:, :], ALU.mult, [bA] + self.bST1, [bA])
            self.act(self.Y[:, c, :], A[:, 0:T], AF.Silu, [bA, self.bPV], self.bY[c],
                     scale=self.pvc("lng", c), bias=self.pvc("lnb", c))

    def ffn_setup(self, base):
        W = T - base
        RWt, Yt = self.RWt, self.Yt
        RWb = RWt[:, :].bitcast(BF16)
        n_in_y = (NCH * T) // W
        n_in_y = min(n_in_y, NJ)
        rest = NJ - n_in_y
        assert rest * W <= 2 * NCH * T
        used_f32 = (rest * W + 1) // 2
        o = used_f32
        npc = TP - base
        GW = 2 + npc + 68
        self.GEXT = RWt[:, o:o + GW]; o += GW
        self.CVb = RWt[:, o:o + W]; o += W
        self.FSTATE = RWt[:, o:o + NJ * 6].rearrange("p (j t) -> p j t", j=NJ); o += NJ * 6
        self.bFSTATE = Buf("FSTATE")
        assert o <= NCH * T, (o, NCH * T)
        self.bGEXT = Buf("GEXT")
        self.bCVb = Buf("CVb")
        self.bHID = [Buf(f"HID{j}") for j in range(NJ)]

        def HID(j):
            if j < n_in_y:
                return Yt[:, j * W:(j + 1) * W]
            jj = j - n_in_y
            return RWb[:, jj * W:(jj + 1) * W]
        self.HID = HID

    def attn_setup(self):
        RWt = self.RWt
        RWb = RWt[:, :].bitcast(BF16)
        o = 0

        def f32(n):
            nonlocal o
            v = RWt[:, o:o + n]
            o += n
            return v

        def b16(n):
            nonlocal o
            m = (n + 1) // 2
            v = RWb[:, 2 * o:2 * o + n]
            o += m
            return v
        self.KT = b16(4 * T).rearrange("p (k t) -> p k t", k=4)
        self.VT = b16(19 * 256).rearrange("p (j f) -> p j f", j=19)
        self.BT = f32(2 * 32 * 64).rearrange("p (b h q) -> p b h q", b=2, h=32)
        self.SE = f32(16 * 64).rearrange("p (h q) -> p h q", h=16)
        self.SB0 = [f32(512) for _ in range(2)]
        self.SB1 = [f32(512) for _ in range(2)]
        self.PT0 = [b16(512) for _ in range(2)]
        self.PT1 = [b16(512) for _ in range(2)]
        self.DEN = [f32(256) for _ in range(2)]
        self.KS = b16(4 * 2 * 160).rearrange("p (k s t) -> p k s t", k=4, s=2)
        self.VSC = b16(2 * 256).rearrange("p (s f) -> p s f", s=2)
        self.VSN = b16(2 * 256).rearrange("p (s f) -> p s f", s=2)
        self.KSTATE = f32(4 * 128).rearrange("p (k t) -> p k t", k=4)
        self.VSTATE = f32(256)
        self.KSNEW = f32(4 * 2 * 32).rearrange("p (k s t) -> p k s t", k=4, s=2)
        self.VSNEWF = f32(2 * 256).rearrange("p (s f) -> p s f", s=2)
        self.CK = f32(4 * 2 * 128).rearrange("p (k s t) -> p k s t", k=4, s=2)
        self.CVI = f32(2 * 256).rearrange("p (s f) -> p s f", s=2)
        self.EB = f32(255)
        self.RB = f32(32)
        self.SE16 = f32(16)
        self.EBB = b16(256)
        self.RBH = b16(32)
        self.RBL = b16(32)
        assert o <= NCH * T
        B = Buf
        self.bKT = [B(f"KT{k}") for k in range(4)]
        self.bVT = [B(f"VT{j}") for j in range(19)]
        self.bBT = B("BT"); self.bSE = B("SE")
        self.bSB0 = [B("SB00"), B("SB01")]; self.bSB1 = [B("SB10"), B("SB11")]
        self.bPT0 = [B("PT00"), B("PT01")]; self.bPT1 = [B("PT10"), B("PT11")]
        self.bDEN = [B("DEN0"), B("DEN1")]
        self.bKS = B("KS"); self.bVSC = B("VSC"); self.bVSN = B("VSN")
        self.bKSTATE = B("KSTATE"); self.bVSTATE = B("VSTATE"); self.bKSNEW = B("KSNEW"); self.bVSNEWF = B("VSNEWF")
        self.bCK = B("CK"); self.bCVI = B("CVI"); self.bEB = B("EB"); self.bRB = B("RB"); self.bSE16 = B("SE16")
        self.bEBB = B("EBB"); self.bRBH = B("RBH"); self.bRBL = B("RBL")
        S = self.S
        S.dma("sp", lambda h: h.dma_start(out=self.CK[:, :, :, :], in_=self.ck), self.d_in, writes=[self.bCK])
        S.dma("sp", lambda h: h.dma_start(out=self.CVI[:, :, :], in_=self.cv), self.d_in, writes=[self.bCVI])
        S.dma("sp", lambda h: h.dma_start(out=self.EB[0:32, :], in_=self.ebase), self.d_in, writes=[self.bEB])
        S.dma("sp", lambda h: h.dma_start(out=self.RB[0:32, :], in_=self.relb), self.d_in, writes=[self.bRB])
        S.finalize([self.bCK, self.bCVI, self.bEB, self.bRB], self.d_in)
        S.dma("sp", lambda h: h.dma_start(out=self.o_kss[:, :, :, 0:96], in_=self.CK[0:64, :, :, 32:128]), self.d_out,
              reads=[self.bCK])
        for s in range(2):
            S.dma("sp", lambda h, s=s: h.dma_start(out=self.o_vss[s, 0:96, :], in_=self.CVI[32:128, s, :]), self.d_out,
                  reads=[self.bCVI])
        self.ts("dve", self.KMB[:, 34:35], self.pvc("hm", 0), -1.0, 30000.0, ALU.add, ALU.mult, [self.bPV], [self.bKMB])
        self.ts("dve", self.KMB[:, 0:34], self.PV[:, PVO["ind"]:PVO["ind"] + 34], self.KMB[:, 34:35], None, ALU.mult, None,
                [self.bPV, self.bKMB], [self.bKMB])
        self.act(self.SE16[:, :], self.PV[:, PVO["sink"]:PVO["sink"] + 16], AF.Exp, [self.bPV], [self.bSE16])
        self.cp("dve", self.SE[:, :, :], self.SE16[:, :].unsqueeze(2).broadcast_to([128, 16, 64]), [self.bSE16], [self.bSE])
        self.cp("pool", self.KS[:, :, :, 0:128], self.CK[:, :, :, :], [self.bCK], [self.bKS])
        self.cp("pool", self.VSC[:, :, :], self.CVI[:, :, :], [self.bCVI], [self.bVSC])
        self.cp("dve", self.EBB[0:32, 0:255], self.EB[0:32, :], [self.bEB], [self.bEBB])
        self.cp("dve", self.RBH[0:32, :], self.RB[0:32, :], [self.bRB], [self.bRBH])
        self.tt("dve", self.RB[0:32, :], self.RB[0:32, :], self.RBH[0:32, :], ALU.subtract, [self.bRB, self.bRBH], [self.bRB])
        self.cp("dve", self.RBL[0:32, :], self.RB[0:32, :], [self.bRB], [self.bRBL])
        for blk in range(2):
            M = 128 if blk == 0 else 64
            for qg in range(4):
                bi = self.bank()
                for qi in range(16):
                    q = qg * 16 + qi
                    off = blk * 128 + 63 - q
                    for part in range(2):
                        rr = self.RBH if part == 0 else self.RBL
                        self.S.op("pe", lambda h, bi=bi, qi=qi, off=off, M=M, rr=rr, part=part: h.matmul(
                            self.PS[bi][0:M, qi * 32:(qi + 1) * 32], lhsT=self.EBB[0:32, off:off + M], rhs=rr[0:32, :],
                            start=(part == 0), stop=(part == 1)),
                            reads=[self.bEBB, self.bRBH, self.bRBL] if (qi == 0 and part == 0) else (),
                            writes=[self.bPS[bi]] if (qi == 0 and part == 0) else (),
                            inc=(qi == 15 and part == 1), skip_self=True)
                self.cp("dve", self.BT[0:M, blk, :, qg * 16:(qg + 1) * 16],
                        self.PS[bi][0:M, 0:512].rearrange("p (q h) -> p h q", q=16), [self.bPS[bi]], [self.bBT])

    def attn_proj_tasks(self):
        S = self.S
        KT_T = _tiles(32, T)
        L1T = _tiles(160, T)

        def hreads(a, b):
            rd = []
            for c in range(NCH):
                rd += self.bufs(self.bH, c, a, b)
            return rd

        def mk_k(kb):
            def load():
                return self.wload([(0, 256, NCH, self.w_k[:, kb * 256:(kb + 1) * 256])])

            def run(s):
                wt = self.WT[s][:, :].rearrange("p (c f) -> p c f", c=NCH)
                for kk in range(2):
                    kvh = kb * 2 + kk
                    for (a, b) in KT_T:
                        w = b - a
                        bi = self.bank()
                        self.mm(bi, self.PS[bi][:, 0:w], [(wt[:, c, kk * 128:(kk + 1) * 128], self.H[:, c, a:b]) for c in range(NCH)],
                                [self.bW[s]] + hreads(a, b))
                        self.act(self.KT[:, kvh, a:b], self.PS[bi][:, 0:w], AF.Copy, [self.bPS[bi]], [self.bKT[kvh]])
                        if b == T:
                            assert a <= 1120
                            self.cp("dve", self.KSTATE[0:64, kvh, :], self.PS[bi][0:64, 1120 - a:1248 - a], [self.bPS[bi]], [self.bKSTATE])
                            self.cp("dve", self.KSNEW[0:64, kvh, :, :], self.PS[bi][0:64, TP - a:T - a].rearrange("p (s t) -> p s t", s=2),
                                    [self.bPS[bi]], [self.bKSNEW])
                    self.cp("pool", self.KS[:, kvh, :, 128:160], self.KT[:, kvh, TP:T].rearrange("p (s t) -> p s t", s=2),
                            [self.bKT[kvh]], [self.bKS])
                if kb == 1:
                    S.dma("sp", lambda h: h.dma_start(out=self.o_kst, in_=self.KSTATE[0:64, :, :]), self.d_out, reads=[self.bKSTATE])
                    S.dma("sp", lambda h: h.dma_start(out=self.o_kss[:, :, :, 96:128], in_=self.KSNEW[0:64, :, :, :]), self.d_out,
                          reads=[self.bKSNEW])
            return load, run
        for kb in range(2):
            ld, rn = mk_k(kb)
            self.task(ld, rn)

        def load_v():
            return self.wload([(0, 256, NCH, self.w_v[:, :])])

        def run_v(s):
            wt = self.WT[s][:, :].rearrange("p (c f) -> p c f", c=NCH)
            for j in range(19):
                t0 = 32 + 64 * j
                bi = self.bank()
                self.mm(bi, self.PS[bi][:, 0:256], [(self.H[:, c, t0:t0 + 128], wt[:, c, :]) for c in range(NCH)],
                        [self.bW[s]] + hreads(t0, t0 + 128))
                self.act(self.VT[:, j, :], self.PS[bi][:, 0:256], AF.Copy, [self.bPS[bi]], [self.bVT[j]])
                if j == 17:
                    self.cp("dve", self.VSTATE[:, :], self.PS[bi][:, 0:256], [self.bPS[bi]], [self.bVSTATE])
                    S.dma("sp", lambda h: h.dma_start(out=self.o_vst, in_=self.VSTATE[:, :]), self.d_out, reads=[self.bVSTATE])
            for sq in range(2):
                t0 = TP + 32 * sq
                bi = self.bank()
                self.mm(bi, self.PS[bi][0:32, 0:256], [(self.H[:, c, t0:t0 + 32], wt[:, c, :]) for c in range(NCH)],
                        [self.bW[s]] + hreads(t0, t0 + 32))
                self.act(self.VSN[0:32, sq, :], self.PS[bi][0:32, 0:256], AF.Copy, [self.bPS[bi]], [self.bVSN])
                self.cp("dve", self.VSNEWF[0:32, sq, :], self.PS[bi][0:32, 0:256], [self.bPS[bi]], [self.bVSNEWF])
            for sq in range(2):
                S.dma("sp", lambda h, sq=sq: h.dma_start(out=self.o_vss[sq, 96:128, :], in_=self.VSNEWF[0:32, sq, :]), self.d_out,
                      reads=[self.bVSNEWF])
        self.task(load_v, run_v)

        def mk_q(blk):
            def load():
                return self.wload([(0, 256, NCH, self.w_q[:, blk * 256:(blk + 1) * 256])])

            def run(s):
                wt = self.WT[s][:, :].rearrange("p (c f) -> p c f", c=NCH)
                for ii in range(2):
                    i = blk * 2 + ii
                    for (a, b) in L1T:
                        w = b - a
                        bi = self.bank()
                        self.mm(bi, self.PS[bi][:, 0:w], [(wt[:, c, ii * 128:(ii + 1) * 128], self.H[:, c, a:b]) for c in range(NCH)],
                                [self.bW[s]] + hreads(a, b))
                        self.act(self.Y[:, i, a:b], self.PS[bi][:, 0:w], AF.Copy, [self.bPS[bi]], self.bufs(self.bY, i, a, b), scale=0.125)
            return load, run
        for blk in range(8):
            ld, rn = mk_q(blk)
            self.task(ld, rn)

    def attn_core(self):
        S = self.S
        it = 0
        jobs = []
        for n in range(17):
            jobs.append(("p", n))
        for sq in range(2):
            jobs.append(("s", sq))
        for job in jobs:
            for kvh in range(4):
                k = it % 2
                it += 1
                if job[0] == "p":
                    n = job[1]
                    q0 = 160 + 64 * n
                    nq = 64
                    kt0 = [self.KT[64 * hf:64 * hf + 64, kvh, q0 - 128:q0] for hf in range(2)]
                    kt1 = [self.KT[64 * hf:64 * hf + 64, kvh, q0:q0 + 64] for hf in range(2)]
                    M1 = 64
                    v0 = self.VT[:, n, kvh * 64:(kvh + 1) * 64]
                    v1 = self.VT[0:64, n + 2, kvh * 64:(kvh + 1) * 64]
                    rv = [self.bVT[n], self.bVT[n + 2]]
                    rk = [self.bKT[kvh]]
                    qsl = [self.Y[64 * hf:64 * hf + 64, 4 * kvh:4 * kvh + 4, q0:q0 + 64] for hf in range(2)]
                    rq = []
                    for c in range(4 * kvh, 4 * kvh + 4):
                        rq += self.bufs(self.bY, c, q0, q0 + 64)
                    m0 = self.KMB[:, 2 * n:2 * n + 1]
                    m1 = self.KMB[0:64, 2 * n + 1:2 * n + 2]
                    ocols = (q0, q0 + 64)
                else:
                    sq = job[1]
                    q0 = TP + 32 * sq
                    nq = 32
                    kt0 = [self.KS[64 * hf:64 * hf + 64, kvh, sq, 0:128] for hf in range(2)]
                    kt1 = [self.KS[64 * hf:64 * hf + 64, kvh, sq, 128:160] for hf in range(2)]
                    M1 = 32
                    v0 = self.VSC[:, sq, kvh * 64:(kvh + 1) * 64]
                    v1 = self.VSN[0:32, sq, kvh * 64:(kvh + 1) * 64]
                    rv = [self.bVSC, self.bVSN]
                    rk = [self.bKS]
                    qsl = [self.Y[64 * hf:64 * hf + 64, 4 * kvh:4 * kvh + 4, q0:q0 + 32] for hf in range(2)]
                    rq = []
                    for c in range(4 * kvh, 4 * kvh + 4):
                        rq += self.bufs(self.bY, c, q0, q0 + 32)
                    m0 = 0.0
                    m1 = 0.0
                    ocols = (q0, q0 + 32)
                NC4 = 4 * nq
                NCOL = 8 * nq
                bx = [self.bank(), self.bank()]
                bo = self.bank(); bs = self.bank()
                for half in range(2):
                    S.op("pe", lambda h, half=half, bx=bx, kt0=kt0, qsl=qsl, NC4=NC4: h.matmul(
                        self.PS[bx[half]][:, 0:NC4], lhsT=kt0[half], rhs=qsl[half], start=True, stop=True),
                        reads=(rk + rq), writes=[self.bPS[bx[half]]], inc=False, skip_self=True)
                    S.op("pe", lambda h, half=half, bx=bx, kt1=kt1, qsl=qsl, M1=M1, NC4=NC4: h.matmul(
                        self.PS[bx[half]][0:M1, NC4:2 * NC4], lhsT=kt1[half], rhs=qsl[half], start=True, stop=True),
                        reads=(), writes=(), inc=True, skip_self=True)
                for half in range(2):
                    hs = kvh * 8 + 4 * half
                    self.tt("dve", self.SB0[k][:, half * NC4:(half + 1) * NC4].rearrange("p (h q) -> p h q", h=4),
                            self.PS[bx[half]][:, 0:NC4].rearrange("p (h q) -> p h q", h=4), self.BT[:, 0, hs:hs + 4, 0:nq], ALU.add,
                            [self.bPS[bx[half]], self.bBT], [self.bSB0[k]])
                    self.tt("dve", self.SB1[k][0:M1, half * NC4:(half + 1) * NC4].rearrange("p (h q) -> p h q", h=4),
                            self.PS[bx[half]][0:M1, NC4:2 * NC4].rearrange("p (h q) -> p h q", h=4), self.BT[0:M1, 1, hs:hs + 4, 0:nq], ALU.add,
                            [self.bPS[bx[half]], self.bBT], [self.bSB1[k]])
                self.act(self.PT0[k][:, 0:NCOL], self.SB0[k][:, 0:NCOL], AF.Exp, [self.bSB0[k], self.bKMB], [self.bPT0[k]], bias=m0)
                self.act(self.PT1[k][0:M1, 0:NCOL], self.SB1[k][0:M1, 0:NCOL], AF.Exp, [self.bSB1[k], self.bKMB], [self.bPT1[k]],
                         bias=(m1 if isinstance(m1, float) else m1[0:M1, :]))
                for half in range(2):
                    for blk in range(2):
                        lhs = v0 if blk == 0 else v1
                        rhs = (self.PT0[k][:, half * NC4:(half + 1) * NC4] if blk == 0
                               else self.PT1[k][0:M1, half * NC4:(half + 1) * NC4])
                        first = (half == 0 and blk == 0)
                        S.op("pe", lambda h, lhs=lhs, rhs=rhs, half=half, blk=blk, bo=bo, NC4=NC4: h.matmul(
                            self.PS[bo][64 * half:64 * half + 64, 0:NC4], lhsT=lhs, rhs=rhs, start=(blk == 0), stop=(blk == 1)),
                            reads=(rv + [self.bPT0[k], self.bPT1[k]]) if first else (), writes=[self.bPS[bo]] if first else (),
                            inc=(half == 1 and blk == 1), skip_self=True)
                for half in range(2):
                    for blk in range(2):
                        lhs = self.ONES[:, 0:64] if blk == 0 else self.ONES[0:M1, 0:64]
                        rhs = (self.PT0[k][:, half * NC4:(half + 1) * NC4] if blk == 0
                               else self.PT1[k][0:M1, half * NC4:(half + 1) * NC4])
                        first = (half == 0 and blk == 0)
                        S.op("pe", lambda h, lhs=lhs, rhs=rhs, half=half, blk=blk, bs=bs, NC4=NC4: h.matmul(
                            self.PS[bs][64 * half:64 * half + 64, 0:NC4], lhsT=lhs, rhs=rhs, start=(blk == 0), stop=(blk == 1)),
                            reads=[self.bPT0[k], self.bPT1[k], self.bONES] if first else (), writes=[self.bPS[bs]] if first else (),
                            inc=(half == 1 and blk == 1), skip_self=True)
                den = self.DEN[k][:, 0:NC4].rearrange("p (h q) -> p h q", h=4)
                self.tt("dve", den, self.PS[bs][:, 0:NC4].rearrange("p (h q) -> p h q", h=4),
                        self.SE[:, kvh * 4:(kvh + 1) * 4, 0:nq], ALU.add, [self.bPS[bs], self.bSE], [self.bDEN[k]])
                S.op("dve", lambda h, k=k, NC4=NC4: h.reciprocal(out=self.DEN[k][:, 0:NC4], in_=self.DEN[k][:, 0:NC4]),
                     reads=[self.bDEN[k]], writes=[self.bDEN[k]])
                wb = []
                for c in range(4 * kvh, 4 * kvh + 4):
                    wb += self.bufs(self.bH, c, ocols[0], ocols[1])
                self.tt("dve", self.H[:, 4 * kvh:4 * kvh + 4, ocols[0]:ocols[1]],
                        self.PS[bo][:, 0:NC4].rearrange("p (h q) -> p h q", h=4), den, ALU.mult,
                        [self.bPS[bo], self.bDEN[k]], wb)


_NC_CACHE = {}


def _get_nc():
    if "nc" not in _NC_CACHE:
        _NC_CACHE["nc"] = Prog().build()
    return _NC_CACHE["nc"]


def _fm(v):
    v = np.asarray(v, np.float32)
    lead = v.shape[:-1]
    n = v.shape[-1] // 128
    v = v.reshape(lead + (n, 128))
    return np.moveaxis(v, -1, 0)


def _prep_shared(inp):
    f = lambda k: np.asarray(inp[k], np.float32)
    sh = {}
    sh["w_mod"] = f("w_mod")
    w_in = f("conv_w_in")[0]
    sh["w_in"] = np.ascontiguousarray(
        np.concatenate([w_in[:, :D].reshape(D, NCH, 1, 128), w_in[:, D:].reshape(D, NCH, 1, 128)], axis=2).reshape(D, 2 * D))
    sh["w_out"] = f("conv_w_out")[0]
    sh["w_q"] = f("attn_w_q")[0]
    wk = f("attn_w_k")[0].reshape(D, 4, 1, 64)
    sh["w_k"] = np.ascontiguousarray(np.concatenate([wk, wk], axis=2).reshape(D, 512))
    sh["w_v"] = f("attn_w_v")[0]
    sh["w_o"] = f("attn_w_o")[0]
    w_up = f("ffn_w_up")
    sh["w_up"] = np.ascontiguousarray(
        np.concatenate([w_up[:, :, :DFF].reshape(2, D, NJ, 1, 128), w_up[:, :, DFF:].reshape(2, D, NJ, 1, 128)], axis=3
                       ).reshape(2, D, 2 * DFF))
    sh["w_down"] = f("ffn_w_down")
    rb = f("rel_bias")
    order = [kvh * 8 + 2 * g + par for kvh in range(4) for par in range(2) for g in range(4)]
    sh["relb"] = np.ascontiguousarray(rb[:, order])
    r = np.arange(255) - 191
    bk = _t5_bucket(r.astype(np.int32))
    eb = np.zeros((32, 255), np.float32)
    eb[bk, np.arange(255)] = 1.0
    sh["ebase"] = eb
    sh["ident"] = np.eye(128, dtype=np.float32)
    return sh


def _prep_core(inp, core):
    f = lambda k: np.asarray(inp[k], np.float32)
    b, seg = core // 4, core % 4
    s0 = seg * NREAL
    xp = f("x_prompt")[b]
    X = np.zeros((T, D), np.float32)
    if seg > 0:
        X[0:HALO] = xp[s0 - HALO:s0]
    X[HALO:TP] = xp[s0:s0 + NREAL]
    X[TP:T] = f("x_sample")[2 * core:2 * core + 2].reshape(64, D)
    m = {}
    m["xin"] = np.ascontiguousarray(X.T.reshape(NCH, 128, T).transpose(1, 0, 2))
    pv = np.zeros((128, NV), np.float32)

    def put(name, arr):
        arr = np.asarray(arr, np.float32).reshape(128, -1)
        pv[:, PVO[name]:PVO[name] + arr.shape[1]] = arr
    put("gnorm", _fm(f("g_norm").reshape(8, D)))
    put("bmod", _fm(f("b_mod")))
    put("cbin", _fm(f("conv_b_in")[0]))
    put("cwdw", _fm(f("conv_w_dw")[0]))
    put("cbdw", _fm(f("conv_b_dw")[0]))
    put("lng", _fm(f("conv_ln_g")[0]))
    put("lnb", _fm(f("conv_ln_b")[0]))
    put("cbout", _fm(f("conv_b_out")[0]))
    put("fwdw", _fm(f("ffn_w_dw")))
    put("fbdw", _fm(f("ffn_b_dw")))
    cs = np.stack([f("c_prompt")[b], f("c_sample")[2 * core], f("c_sample")[2 * core + 1]])
    put("cT", _fm(cs))
    pv[:, PVO["hm"]] = 1.0 if seg > 0 else 0.0
    sinks = f("attn_sinks")[0]
    sl = np.zeros((128, 16), np.float32)
    for kvh in range(4):
        for g in range(4):
            sl[0:64, kvh * 4 + g] = sinks[kvh * 8 + 2 * g]
            sl[64:128, kvh * 4 + g] = sinks[kvh * 8 + 2 * g + 1]
    put("sink", sl)
    ind = np.zeros((128, 34), np.float32)
    p = np.arange(128)
    for n in range(17):
        q0 = 160 + 64 * n
        ind[:, 2 * n] = ((q0 - 128 + p) < HALO)
        ind[:, 2 * n + 1] = ((q0 + p) < HALO)
    put("ind", ind)
    m["pvec"] = pv
    cc = f("cache_conv")[0, 2 * core:2 * core + 2]
    m["cconv"] = np.ascontiguousarray(cc.reshape(2, 30, NCH, 128).transpose(3, 2, 0, 1))
    ck = f("cache_k")[0, 2 * core:2 * core + 2]
    ckt = ck.transpose(3, 2, 0, 1)
    m["ck"] = np.ascontiguousarray(np.concatenate([ckt, ckt], axis=0))
    cv = f("cache_v")[0, 2 * core:2 * core + 2]
    m["cv"] = np.ascontiguousarray(cv.reshape(2, 128, 256).transpose(1, 0, 2))
    cf = f("cache_ffn")[:, 2 * core:2 * core + 2]
    m["cffn"] = np.ascontiguousarray(cf.reshape(2, 2, 2, NJ, 128).transpose(4, 0, 3, 1, 2))
    return m


def _run(inp, cores=None):
    nc = _get_nc()
    sh = _prep_shared(inp)
    cores = list(range(NCORES)) if cores is None else cores
    in_maps = []
    for c in cores:
        m = _prep_core(inp, c)
        m.update(sh)
        in_maps.append(m)
    res = run_bass_kernel_spmd(nc, in_maps, core_ids=list(range(len(cores))))
    return res.results


def _assemble(inp, results):
    B, SEQ, DB = 2, 4096, 16
    y_p = np.zeros((B, SEQ, D), np.float32)
    y_s = np.zeros((DB, 32, D), np.float32)
    cs_p = np.zeros((1, B, 30, D), np.float32)
    cs_s = np.zeros((1, DB, 30, D), np.float32)
    k_p = np.zeros((1, B, 128, 4, 64), np.float32)
    v_p = np.zeros((1, B, 128, 4, 64), np.float32)
    k_s = np.zeros((1, DB, 128, 4, 64), np.float32)
    v_s = np.zeros((1, DB, 128, 4, 64), np.float32)
    f_p = np.zeros((2, B, 2, DFF), np.float32)
    f_s = np.zeros((2, DB, 2, DFF), np.float32)
    for core, r in enumerate(results):
        b, seg = core // 4, core % 4
        oy = np.asarray(r["o_y"])
        yt = oy.transpose(2, 1, 0).reshape(NOUT, D)
        y_p[b, seg * NREAL:(seg + 1) * NREAL] = yt[:NREAL]
        y_s[2 * core:2 * core + 2] = yt[NREAL:].reshape(2, 32, D)
        ust = np.asarray(r["o_ust"]).transpose(2, 1, 0).reshape(90, D)
        cs_s[0, 2 * core] = ust[30:60]
        cs_s[0, 2 * core + 1] = ust[60:90]
        kss = np.asarray(r["o_kss"])
        k_s[0, 2 * core:2 * core + 2] = kss.transpose(2, 3, 1, 0)
        vss = np.asarray(r["o_vss"])
        v_s[0, 2 * core:2 * core + 2] = vss.reshape(2, 128, 4, 64)
        fst = np.asarray(r["o_fst"])
        ft = fst.transpose(1, 3, 2, 0).reshape(2, 6, DFF)
        f_s[:, 2 * core] = ft[:, 2:4]
        f_s[:, 2 * core + 1] = ft[:, 4:6]
        if seg == 3:
            cs_p[0, b] = ust[0:30]
            k_p[0, b] = np.asarray(r["o_kst"]).transpose(2, 1, 0)
            v_p[0, b] = np.asarray(r["o_vst"]).reshape(128, 4, 64)
            f_p[:, b] = ft[:, 0:2]
    return (y_p, y_s, cs_p, cs_s, k_p, v_p, k_s, v_s, f_p, f_s)


def kernel(**inputs):
    results = _run(inputs)
    return _assemble(inputs, results)
```

```python
from contextlib import ExitStack
import numpy as np
import concourse.bass as bass
import concourse.mybir as mybir
from concourse.bass_utils import run_bass_kernel_spmd

F32 = mybir.dt.float32
BF16 = mybir.dt.bfloat16
AF = mybir.ActivationFunctionType
ALU = mybir.AluOpType

D = 2048
NCH = 16
DFF = 5632
NJ = 44
T = 1312
TP = 1248
HALO = 224
NREAL = 1024
NOUT = 1088
EPS = 1e-6
NCORES = 8
SEQR = [(0, 1248, 0), (1248, 1280, 1), (1280, 1312, 2)]


class Buf:
    __slots__ = ("name", "last_w", "readers", "excl")

    def __init__(self, name, excl=False):
        self.name = name
        self.last_w = None
        self.readers = []
        self.excl = excl


class DmaSem:
    def __init__(self, sem):
        self.sem = sem
        self.count = 0


class Eng:
    def __init__(self, name, sem):
        self.name = name
        self.sem = sem
        self.count = 0
        self.ops = []
        self.waited = {}
        self.pending = False


class Sched:
    def __init__(self, nc, ctx):
        self.nc = nc
        self.ctx = ctx
        self.engs = {}
        for n in ("pe", "act", "dve", "pool", "sp"):
            self.engs[n] = Eng(n, ctx.enter_context(nc.semaphore("s_" + n)))
        self.dsems = []

    def dma_sem(self, name):
        d = DmaSem(self.ctx.enter_context(self.nc.semaphore(name)))
        self.dsems.append(d)
        return d

    def _deps(self, reads, writes):
        deps = []
        for b in reads:
            if b.last_w is not None:
                deps.append(b.last_w)
            if b.excl:
                deps.extend(b.readers)
        for b in writes:
            if b.last_w is not None:
                deps.append(b.last_w)
            deps.extend(b.readers)
        return deps

    def _waits(self, e, deps, skip_self=False):
        need = {}
        for (kind, obj, val) in deps:
            if kind == "eng" and skip_self and obj is e:
                continue
            sem = obj.sem
            k = id(sem)
            if e.waited.get(k, 0) >= val:
                continue
            if k not in need or need[k][1] < val:
                need[k] = (sem, val)
        out = []
        for k, (sem, val) in need.items():
            e.waited[k] = val
            out.append((sem, val))
        return out

    def op(self, eng, fn, reads=(), writes=(), inc=True, skip_self=False):
        e = self.engs[eng]
        waits = self._waits(e, self._deps(reads, writes), skip_self=skip_self)
        e.ops.append((waits, fn, ("eng", inc)))
        val = e.count + 1
        if inc:
            e.count += 1
            e.pending = False
        else:
            e.pending = True
        tag = ("eng", e, val)
        for b in writes:
            b.last_w = tag
            b.readers = []
        for b in reads:
            b.readers.append(tag)

    def dma(self, eng, fn, dsem, reads=(), writes=()):
        e = self.engs[eng]
        waits = self._waits(e, self._deps(reads, writes))
        e.ops.append((waits, fn, ("dma", dsem)))
        dsem.count += 16
        tag = ("dma", dsem, dsem.count)
        for b in writes:
            b.last_w = tag
            b.readers = []
        for b in reads:
            b.readers.append(tag)

    def finalize(self, bufs, dsem):
        tag = ("dma", dsem, dsem.count)
        for b in bufs:
            b.last_w = tag

    def barrier(self, names=("pe", "act", "dve", "pool", "sp"), dsems=None):
        for n in names:
            assert not self.engs[n].pending
        targets = [(self.engs[n].sem, self.engs[n].count) for n in names if self.engs[n].count > 0]
        for d in (self.dsems if dsems is None else dsems):
            if d.count > 0:
                targets.append((d.sem, d.count))
        for n in names:
            e = self.engs[n]
            waits = []
            for sem, val in targets:
                if sem is e.sem:
                    continue
                k = id(sem)
                if e.waited.get(k, 0) >= val:
                    continue
                e.waited[k] = val
                waits.append((sem, val))
            if waits:
                e.ops.append((waits, None, None))

    def wait_all(self, eng, dsems):
        e = self.engs[eng]
        waits = [(d.sem, d.count) for d in dsems if d.count > 0]
        e.ops.append((waits, None, None))

    def emit(self):
        nc = self.nc
        for n, e in self.engs.items():
            assert not e.pending, f"engine {n} has trailing non-inc ops"
        with nc.Block() as block:
            def run(e, h):
                for waits, fn, kind in e.ops:
                    for sem, val in waits:
                        h.wait_ge(sem, val)
                    if fn is None:
                        continue
                    ins = fn(h)
                    if kind[0] == "eng":
                        if kind[1]:
                            ins.then_inc(e.sem, 1)
                    else:
                        ins.then_inc(kind[1].sem, 16)

            @block.tensor
            def _(h):
                run(self.engs["pe"], h)

            @block.scalar
            def _(h):
                run(self.engs["act"], h)

            @block.vector
            def _(h):
                run(self.engs["dve"], h)

            @block.gpsimd
            def _(h):
                run(self.engs["pool"], h)

            @block.sync
            def _(h):
                run(self.engs["sp"], h)


def _pv_layout():
    off = {}
    n = 0
    for name, cols in [("gnorm", 8 * 16), ("bmod", 2 * 96), ("cbin", 32), ("cwdw", 31 * 16), ("cbdw", 16),
                       ("lng", 16), ("lnb", 16), ("cbout", 16), ("fwdw", 2 * 3 * NJ), ("fbdw", 2 * NJ),
                       ("cT", 48), ("hm", 1), ("sink", 16), ("ind", 34)]:
        off[name] = n
        n += cols
    return off, n


PVO, NV = _pv_layout()


def _t5_bucket(rel):
    n_buckets, max_distance = 32, 128
    half = n_buckets // 2
    max_exact = half // 2
    n = np.abs(rel)
    ret = np.where(rel > 0, half, 0)
    nf = np.maximum(n, 1).astype(np.float32)
    large = max_exact + (np.log(nf / max_exact) / np.float32(np.log(max_distance / max_exact))
                         * (half - max_exact)).astype(np.int32)
    large = np.minimum(large, half - 1)
    return ret + np.where(n < max_exact, n, large)


def _tiles(a, b, w=512):
    out = []
    while a < b:
        e = min(a + w, b)
        out.append((a, e))
        a = e
    return out


def _seq_pieces(a, b):
    out = []
    for (s0, s1, s) in SEQR:
        lo, hi = max(a, s0), min(b, s1)
        if lo < hi:
            out.append((lo, hi, s))
    return out


class Prog:
    def __init__(self):
        self.nc = bass.Bass("TRN2", target_bir_lowering=False)
        self.tasks = []

    def dram(self):
        nc = self.nc
        I = lambda n, s: nc.dram_tensor(n, s, F32, kind="ExternalInput").ap()
        O = lambda n, s: nc.dram_tensor(n, s, F32, kind="ExternalOutput").ap()
        self.xin = I("xin", [128, NCH, T])
        self.pvec = I("pvec", [128, NV])
        self.cconv = I("cconv", [128, NCH, 2, 30])
        self.ck = I("ck", [128, 4, 2, 128])
        self.cv = I("cv", [128, 2, 256])
        self.cffn = I("cffn", [128, 2, NJ, 2, 2])
        self.relb = I("relb", [32, 32])
        self.ebase = I("ebase", [32, 255])
        self.ident = I("ident", [128, 128])
        self.w_mod = I("w_mod", [2, D, 6 * D])
        self.w_in = I("w_in", [D, 2 * D])
        self.w_out = I("w_out", [D, D])
        self.w_q = I("w_q", [D, D])
        self.w_k = I("w_k", [D, 512])
        self.w_v = I("w_v", [D, 256])
        self.w_o = I("w_o", [D, D])
        self.w_up = I("w_up", [2, D, 2 * DFF])
        self.w_down = I("w_down", [2, DFF, D])
        self.o_y = O("o_y", [128, NCH, NOUT])
        self.o_ust = O("o_ust", [128, NCH, 90])
        self.o_kst = O("o_kst", [64, 4, 128])
        self.o_vst = O("o_vst", [128, 256])
        self.o_kss = O("o_kss", [64, 4, 2, 128])
        self.o_vss = O("o_vss", [2, 128, 256])
        self.o_fst = O("o_fst", [128, 2, NJ, 6])
        self.rs = nc.dram_tensor("rs", [128, NCH, T], F32, kind="Internal").ap()

    def alloc(self, ctx):
        nc = self.nc
        S = self.S
        sb = lambda n, s, dt: ctx.enter_context(nc.sbuf_tensor(n, s, dt))
        self.H = sb("H", [128, NCH, T], BF16)
        self.Yt = sb("Y", [128, NCH * T], BF16)
        self.Y = self.Yt[:, :].rearrange("p (c t) -> p c t", c=NCH)
        self.RWt = sb("RW", [128, NCH * T], F32)
        self.RW = self.RWt[:, :].rearrange("p (c t) -> p c t", c=NCH)
        self.WT = [sb(f"WT{i}", [128, 4096], BF16) for i in range(3)]
        self.PV = sb("PV", [128, NV], F32)
        self.MODT = sb("MODT", [128, 2, 96, 3], F32)
        self.DER = sb("DER", [128, 2, 4, NCH, 3], F32)
        self.ONES = sb("ONES", [128, 128], BF16)
        self.SCB = sb("SCB", [128, NCH, 3], BF16)
        self.ST1 = sb("ST1", [128, T], F32)
        self.SQ = [sb(f"SQ{i}", [128, 512], BF16) for i in range(2)]
        self.CFFN = sb("CFFN", [128, 2, NJ, 2, 2], F32)
        self.KMB = sb("KMB", [128, 36], F32)
        self.PS = [ctx.enter_context(nc.psum_tensor(f"ps{i}", [128, 512], F32)) for i in range(8)]
        self.bH = [[Buf(f"H{c}_{t}") for t in range(3)] for c in range(NCH)]
        self.bY = [[Buf(f"Y{c}_{t}") for t in range(3)] for c in range(NCH)]
        self.bRW = [[Buf(f"RW{c}_{t}") for t in range(3)] for c in range(NCH)]
        self.bW = [Buf("W0"), Buf("W1"), Buf("W2")]
        self.dW = [S.dma_sem("dW0"), S.dma_sem("dW1"), S.dma_sem("dW2")]
        self.bPS = [Buf(f"PS{i}", excl=True) for i in range(8)]
        self.bPV = Buf("PV")
        self.bMODT = Buf("MODT")
        self.bDER = Buf("DER")
        self.bONES = Buf("ONES")
        self.bSCB = Buf("SCB")
        self.bST1 = [Buf(f"ST1_{t}") for t in range(3)]
        self.bST2 = [Buf(f"ST2_{t}") for t in range(3)]
        self.bSQ = [Buf("SQ0"), Buf("SQ1")]
        self.bCFFN = Buf("CFFN")
        self.bKMB = Buf("KMB")
        self.d_in = S.dma_sem("d_in")
        self.d_out = S.dma_sem("d_out")
        self.d_rs = S.dma_sem("d_rs")
        self.d_xs = [S.dma_sem(f"d_xs{i}") for i in range(4)]
        self.sq_i = 0
        self.rot = list(range(8))
        self.rot_i = 0
        self.wslot = 0

    @staticmethod
    def tix(a, b):
        return list(range(a // 512, (b - 1) // 512 + 1))

    def bufs(self, table, c, a, b):
        return [table[c][t] for t in self.tix(a, b)]

    def pin(self, idxs):
        self.rot = [i for i in self.rot if i not in idxs]
        self.rot_i = 0

    def unpin(self, idxs):
        self.rot = sorted(set(self.rot) | set(idxs))
        self.rot_i = 0

    def bank(self):
        i = self.rot[self.rot_i % len(self.rot)]
        self.rot_i += 1
        return i

    def mm(self, bi, out_ap, pairs, reads, first=True, last=True):
        n = len(pairs)
        S = self.S
        for i, (l, r) in enumerate(pairs):
            S.op("pe",
                 lambda h, l=l, r=r, st=(first and i == 0), sp=(last and i == n - 1): h.matmul(out_ap, lhsT=l, rhs=r, start=st, stop=sp),
                 reads=reads if i == 0 else (), writes=[self.bPS[bi]] if i == 0 else (),
                 inc=(i == n - 1), skip_self=True)

    def act(self, out, in_, func, reads, writes, scale=1.0, bias=0.0):
        self.S.op("act", lambda h: h.activation(out=out, in_=in_, func=func, scale=scale, bias=bias),
                  reads=reads, writes=writes)

    def tt(self, eng, out, in0, in1, op, reads, writes):
        self.S.op(eng, lambda h: h.tensor_tensor(out=out, in0=in0, in1=in1, op=op), reads=reads, writes=writes)

    def ts(self, eng, out, in0, s1, s2, op0, op1, reads, writes):
        if s2 is None:
            self.S.op(eng, lambda h: h.tensor_scalar(out=out, in0=in0, scalar1=s1, scalar2=None, op0=op0),
                      reads=reads, writes=writes)
        else:
            self.S.op(eng, lambda h: h.tensor_scalar(out=out, in0=in0, scalar1=s1, scalar2=s2, op0=op0, op1=op1),
                      reads=reads, writes=writes)

    def stt(self, out, in0, scalar, in1, op0, op1, reads, writes):
        self.S.op("dve", lambda h: h.scalar_tensor_tensor(out=out, in0=in0, scalar=scalar, in1=in1, op0=op0, op1=op1),
                  reads=reads, writes=writes)

    def cp(self, eng, out, in_, reads, writes):
        self.S.op(eng, lambda h: h.tensor_copy(out=out, in_=in_), reads=reads, writes=writes)

    def pvc(self, name, col):
        o = PVO[name] + col
        return self.PV[:, o:o + 1]

    def task(self, load, run):
        self.tasks.append((load, run))

    def wload(self, parts):
        s = self.wslot
        self.wslot = (self.wslot + 1) % 3
        for (dst_off, ncol, nchk, src) in parts:
            dst = self.WT[s][:, dst_off:dst_off + nchk * ncol].rearrange("p (c f) -> p c f", c=nchk)
            srcv = src.rearrange("(c p) f -> p c f", p=128)
            self.S.dma("pool", lambda h, dst=dst, srcv=srcv: h.dma_start(out=dst, in_=srcv), self.dW[s],
                       writes=[self.bW[s]])
        return s

    def run_tasks(self):
        tasks = self.tasks
        slots = {}
        loads = [i for i, (ld, _) in enumerate(tasks) if ld is not None]
        li = 0
        for i, (ld, run) in enumerate(tasks):
            if ld is not None:
                while li < len(loads) and loads[li] <= i:
                    k = loads[li]
                    slots[k] = tasks[k][0]()
                    li += 1
                ahead = 0
                for k2 in loads[li:li + 2]:
                    ahead += 1
                while li < len(loads) and sum(1 for k2 in loads[:li] if k2 > i) < 2:
                    k = loads[li]
                    slots[k] = tasks[k][0]()
                    li += 1
            run(slots.get(i))

    def sq_buf(self):
        i = self.sq_i
        self.sq_i ^= 1
        return i

    def flush(self):
        f = getattr(self, "deferred", None)
        self.deferred = None
        if f is not None:
            f()

    def stat_mm(self, bank_i, ncols, rhs, reads, first, last):
        self.mm(bank_i, self.PS[bank_i][:, 0:ncols], [(self.ONES[:, :], rhs)], reads + [self.bONES], first=first, last=last)

    def rstd_from(self, tiles, banks, st, bst):
        for ti, (a, b) in enumerate(tiles):
            w = b - a
            for t in self.tix(a, b):
                pass
            bw = [bst[t] for t in self.tix(a, b)]
            self.act(st[:, a:b], self.PS[banks[ti]][:, 0:w], AF.Sqrt, [self.bPS[banks[ti]]], bw, scale=1.0 / D, bias=EPS)
            self.S.op("dve", lambda h, a=a, b=b: h.reciprocal(out=st[:, a:b], in_=st[:, a:b]), reads=bw, writes=bw)

    def x_stats(self, tiles):
        banks = [5, 6, 7]
        self.pin(banks)
        for c in range(NCH):
            for ti, (a, b) in enumerate(tiles):
                q = self.sq_buf()
                w = b - a
                self.act(self.SQ[q][:, 0:w], self.RW[:, c, a:b], AF.Square, self.bufs(self.bRW, c, a, b), [self.bSQ[q]])
                self.stat_mm(banks[ti], w, self.SQ[q][:, 0:w], [self.bSQ[q]], first=(c == 0), last=(c == NCH - 1))
        self.rstd_from(tiles, banks, self.ST1, self.bST1)
        self.unpin(banks)

    def norm_apply(self, tiles, layer, ka, kb_which):
        TN = self.TN
        for c in range(NCH):
            for ti, (a, b) in enumerate(tiles):
                k = (c * 3 + ti) % 2
                w = b - a
                self.tt("dve", TN[k][:, 0:w], self.RW[:, c, a:b], self.ST1[:, a:b], ALU.mult,
                        self.bufs(self.bRW, c, a, b) + self.bufs_st(self.bST1, a, b), [self.bTN[k]])
                for (lo, hi, s) in _seq_pieces(a, b):
                    self.act(self.H[:, c, lo:hi], TN[k][:, lo - a:hi - a], AF.Identity,
                             [self.bTN[k], self.bDER, self.bMODT], self.bufs(self.bH, c, lo, hi),
                             scale=self.DER[:, layer, ka, c, s:s + 1],
                             bias=self.MODT[:, layer, kb_which * 16 + c, s:s + 1])

    def bufs_st(self, table, a, b):
        return [table[t] for t in self.tix(a, b)]

    def mod_tasks(self, layer, blk_lo, blk_hi, modbank):
        def mk(blk):
            def load():
                return self.wload([(0, 256, NCH, self.w_mod[layer, :, blk * 256:(blk + 1) * 256])])

            def run(s):
                wt = self.WT[s][:, :].rearrange("p (c f) -> p c f", c=NCH)
                for jj in range(2):
                    jc = blk * 2 + jj
                    pairs = [(wt[:, c, jj * 128:(jj + 1) * 128], self.SCB[:, c, :]) for c in range(NCH)]
                    self.mm(modbank, self.PS[modbank][:, jc * 3:jc * 3 + 3], pairs, [self.bW[s], self.bSCB])
            return load, run
        for blk in range(blk_lo, blk_hi):
            ld, rn = mk(blk)
            self.task(ld, rn)

    def mod_evac(self, layer, jlo, jhi, modbank):
        o = PVO["bmod"] + layer * 96
        n = jhi - jlo
        self.tt("dve", self.MODT[:, layer, jlo:jhi, :],
                self.PS[modbank][:, jlo * 3:jhi * 3].rearrange("p (j s) -> p j s", s=3),
                self.PV[:, o + jlo:o + jhi].unsqueeze(2).broadcast_to([128, n, 3]), ALU.add,
                [self.bPS[modbank], self.bPV], [self.bMODT])

    def mod_derive(self, layer, which_scale, gidx, kder, is_gate):
        g = self.PV[:, PVO["gnorm"] + (layer * 4 + gidx) * 16: PVO["gnorm"] + (layer * 4 + gidx + 1) * 16]
        gb = g.unsqueeze(2).broadcast_to([128, NCH, 3])
        src = self.MODT[:, layer, which_scale * 16:(which_scale + 1) * 16, :]
        dst = self.DER[:, layer, kder, :, :]
        if is_gate:
            self.tt("dve", dst, src, gb, ALU.mult, [self.bMODT, self.bPV], [self.bDER])
        else:
            self.stt(dst, src, 1.0, gb, ALU.add, ALU.mult, [self.bMODT, self.bPV], [self.bDER])

    def resid_update(self, a0, a1, layer, kg, src_dram, fsrc=None, after=None):
        XS = self.XS
        for c in range(NCH):
            k = c % 4
            self.S.dma("sp", lambda h, c=c, k=k: h.dma_start(out=XS[k][:, a0:a1], in_=src_dram[:, c, a0:a1]),
                       self.d_xs[k], writes=[self.bXS[k]])
            for (lo, hi, s) in _seq_pieces(a0, a1):
                g = self.DER[:, layer, kg, c, s:s + 1]
                if fsrc is None:
                    self.stt(self.RW[:, c, lo:hi], self.RW[:, c, lo:hi], g, self.ST1[:, lo:hi], ALU.mult, ALU.mult,
                             self.bufs(self.bRW, c, lo, hi) + self.bufs_st(self.bST1, lo, hi) + [self.bDER],
                             self.bufs(self.bRW, c, lo, hi))
                else:
                    self.stt(self.RW[:, c, lo:hi], fsrc[:, c, lo:hi], g, self.ST1[:, lo:hi], ALU.mult, ALU.mult,
                             self.bufs(self.bH, c, lo, hi) + self.bufs_st(self.bST1, lo, hi) + [self.bDER],
                             self.bufs(self.bRW, c, lo, hi))
            self.tt("dve", self.RW[:, c, a0:a1], self.RW[:, c, a0:a1], XS[k][:, a0:a1], ALU.add,
                    self.bufs(self.bRW, c, a0, a1) + [self.bXS[k]], self.bufs(self.bRW, c, a0, a1))
            if after is not None:
                after(c)

    def proj_stats_tasks(self, w_ap, tiles, bias_name, src, bsrc):
        banks = [5, 6, 7]

        def mk(blk):
            def load():
                return self.wload([(0, 256, NCH, w_ap[:, blk * 256:(blk + 1) * 256])])

            def run(s):
                if blk == 0:
                    self.pin(banks)
                wt = self.WT[s][:, :].rearrange("p (c f) -> p c f", c=NCH)
                for ii in range(2):
                    i = blk * 2 + ii
                    for ti, (a, b) in enumerate(tiles):
                        w = b - a
                        bi = self.bank()
                        pairs = [(wt[:, c, ii * 128:(ii + 1) * 128], src[:, c, a:b]) for c in range(NCH)]
                        rd = [self.bW[s]]
                        for c in range(NCH):
                            rd += self.bufs(bsrc, c, a, b)
                        self.mm(bi, self.PS[bi][:, 0:w], pairs, rd)
                        self.flush()
                        bias = self.pvc(bias_name, i) if bias_name else 0.0
                        rdb = [self.bPS[bi]] + ([self.bPV] if bias_name else [])
                        self.act(self.RW[:, i, a:b], self.PS[bi][:, 0:w], AF.Identity, rdb, self.bufs(self.bRW, i, a, b), bias=bias)
                        q = self.sq_buf()
                        self.act(self.SQ[q][:, 0:w], self.PS[bi][:, 0:w], AF.Square, rdb, [self.bSQ[q]], bias=bias)
                        self.deferred = (lambda ti=ti, w=w, q=q, i=i: self.stat_mm(
                            banks[ti], w, self.SQ[q][:, 0:w], [self.bSQ[q]], first=(i == 0), last=(i == NCH - 1)))
                if blk == 7:
                    self.flush()
                    self.rstd_from(tiles, banks, self.ST1, self.bST1)
                    self.unpin(banks)
            return load, run
        for blk in range(8):
            ld, rn = mk(blk)
            self.task(ld, rn)

    def ffn_tasks(self, layer, base, mod_next=None):
        S = self.S
        W = T - base
        tiles = _tiles(base, T)
        npc = TP - base
        GW = 2 + npc + 68

        def gidx(col):
            if col < TP:
                return 2 + col - base
            s = (col - TP) // 32
            return 2 + npc + s * 34 + 2 + (col - TP - s * 32)

        def mk_up(j):
            def load():
                return self.wload([(0, 256, NCH, self.w_up[layer, :, j * 256:(j + 1) * 256])])

            def run(s):
                HID = self.HID
                GEXT, CVb = self.GEXT, self.CVb
                bG, bC = self.bGEXT, self.bCVb
                bHID = self.bHID
                wt = self.WT[s][:, :].rearrange("p (c f) -> p c f", c=NCH)
                if j == 0:
                    S.op("pool", lambda h: h.memset(GEXT[:, 0:2], 0.0), writes=[bG])
                self.cp("pool", GEXT[:, 2 + npc:2 + npc + 68].rearrange("p (s t) -> p s t", s=2)[:, :, 0:2],
                        self.CFFN[:, layer, j, :, :], [self.bCFFN], [bG])
                vb = []
                for ti, (a, b) in enumerate(tiles):
                    w = b - a
                    rd = [self.bW[s]]
                    for c in range(NCH):
                        rd += self.bufs(self.bH, c, a, b)
                    bg = self.bank()
                    self.mm(bg, self.PS[bg][:, 0:w], [(wt[:, c, 0:128], self.H[:, c, a:b]) for c in range(NCH)], rd)
                    bv = self.bank()
                    self.mm(bv, self.PS[bv][:, 0:w], [(wt[:, c, 128:256], self.H[:, c, a:b]) for c in range(NCH)], rd)
                    vb.append(bv)
                    pa, pb = a, min(b, TP)
                    if pa < pb:
                        self.act(GEXT[:, gidx(pa):gidx(pa) + (pb - pa)], self.PS[bg][:, pa - a:pb - a], AF.Copy, [self.bPS[bg]], [bG])
                    if b > TP:
                        sa = max(a, TP)
                        assert sa == TP and b == T
                        self.act(GEXT[:, 2 + npc:2 + npc + 68].rearrange("p (s t) -> p s t", s=2)[:, :, 2:34],
                                 self.PS[bg][:, TP - a:T - a].rearrange("p (s t) -> p s t", s=2), AF.Copy, [self.bPS[bg]], [bG])
                self.ts("dve", GEXT[:, gidx(222):gidx(222) + 2], GEXT[:, gidx(222):gidx(222) + 2], self.pvc("hm", 0), None,
                        ALU.mult, None, [bG, self.bPV], [bG])
                self.cp("pool", self.FSTATE[:, j, 0:2], GEXT[:, gidx(1246):gidx(1246) + 2], [bG], [self.bFSTATE])
                self.cp("pool", self.FSTATE[:, j, 2:6].rearrange("p (s t) -> p s t", s=2),
                        GEXT[:, 2 + npc:2 + npc + 68].rearrange("p (s t) -> p s t", s=2)[:, :, 32:34], [bG], [self.bFSTATE])
                wo = PVO["fwdw"] + layer * 3 * NJ
                wk = [self.PV[:, wo + k * NJ + j: wo + k * NJ + j + 1] for k in range(3)]
                bo = self.PV[:, PVO["fbdw"] + layer * NJ + j: PVO["fbdw"] + layer * NJ + j + 1]
                for part in range(2):
                    if part == 0:
                        src = lambda k: GEXT[:, k:k + npc]
                        dst = CVb[:, 0:npc]
                    else:
                        src = lambda k: GEXT[:, 2 + npc:2 + npc + 68].rearrange("p (s t) -> p s t", s=2)[:, :, k:k + 32]
                        dst = CVb[:, npc:npc + 64].rearrange("p (s t) -> p s t", s=2)
                    self.ts("dve", dst, src(0), wk[0], bo, ALU.mult, ALU.add, [bG, self.bPV], [bC])
                    self.stt(dst, src(1), wk[1], dst, ALU.mult, ALU.add, [bG, bC, self.bPV], [bC])
                    self.stt(dst, src(2), wk[2], dst, ALU.mult, ALU.add, [bG, bC, self.bPV], [bC])
                self.act(CVb[:, 0:W], CVb[:, 0:W], AF.Gelu_apprx_tanh, [bC], [bC])
                for ti, (a, b) in enumerate(tiles):
                    self.tt("dve", HID(j)[:, a - base:b - base], CVb[:, a - base:b - base], self.PS[vb[ti]][:, 0:b - a], ALU.mult,
                            [bC, self.bPS[vb[ti]]], [bHID[j]])
                if j == NJ - 1:
                    S.dma("sp", lambda h, fs=self.FSTATE: h.dma_start(out=self.o_fst[:, layer, :, :], in_=fs[:, :, :]), self.d_out,
                          reads=[self.bFSTATE])
            return load, run

        stat_banks = [5, 6, 7]

        def mk_down(i, hh):
            def load():
                return self.wload([(0, 128, 22, self.w_down[layer, hh * 2816:(hh + 1) * 2816, i * 128:(i + 1) * 128])])

            def run(s):
                HID = self.HID
                bHID = self.bHID
                if i == 0 and hh == 0:
                    self.pin(stat_banks)
                wt = self.WT[s][:, 0:22 * 128].rearrange("p (c f) -> p c f", c=22)
                if hh == 0:
                    self.dbanks = [self.bank() for _ in tiles]
                for ti, (a, b) in enumerate(tiles):
                    w = b - a
                    bi = self.dbanks[ti]
                    pairs = [(wt[:, jj, :], HID(hh * 22 + jj)[:, a - base:b - base]) for jj in range(22)]
                    rd = [self.bW[s]] + [bHID[hh * 22 + jj] for jj in range(22)]
                    self.mm(bi, self.PS[bi][:, 0:w], pairs, rd, first=(hh == 0), last=(hh == 1))
                    self.flush()
                    if hh == 1:
                        self.act(self.H[:, i, a:b], self.PS[bi][:, 0:w], AF.Copy, [self.bPS[bi]], self.bufs(self.bH, i, a, b))
                        q = self.sq_buf()
                        self.act(self.SQ[q][:, 0:w], self.PS[bi][:, 0:w], AF.Square, [self.bPS[bi]], [self.bSQ[q]])
                        self.deferred = (lambda ti=ti, w=w, q=q, i=i: self.stat_mm(
                            stat_banks[ti], w, self.SQ[q][:, 0:w], [self.bSQ[q]], first=(i == 0), last=(i == NCH - 1)))
                if i == NCH - 1 and hh == 1:
                    self.flush()
                    self.rstd_from(tiles, stat_banks, self.ST1, self.bST1)
                    self.unpin(stat_banks)
            return load, run

        for j in range(NJ):
            ld, rn = mk_up(j)
            self.task(ld, rn)
            if mod_next is not None:
                mod_next(j)
        for i in range(NCH):
            for hh in range(2):
                ld, rn = mk_down(i, hh)
                self.task(ld, rn)

    def build(self):
        nc = self.nc
        self.dram()
        with ExitStack() as ctx:
            self.S = S = Sched(nc, ctx)
            self.alloc(ctx)
            self.program()
            self.run_tasks()
            S.barrier()
            S.wait_all("sp", S.dsems)
            S.emit()
        return nc

    def program(self):
        S = self.S
        nc = self.nc
        RWt, Yt = self.RWt, self.Yt
        Yf = Yt[:, :].bitcast(F32)
        ALLT = _tiles(0, T)
        MODB = 4

        self.TN = [Yf[:, 0:512], Yf[:, 512:1024]]
        self.bTN = [Buf("TN0"), Buf("TN1")]
        self.XS = [Yf[:, 1024 + i * T:1024 + (i + 1) * T] for i in range(4)]
        self.bXS = [Buf(f"XS{i}") for i in range(4)]

        def p0(_):
            S.dma("sp", lambda h: h.dma_start(out=self.PV[:, :], in_=self.pvec), self.d_in, writes=[self.bPV])
            for c in range(NCH):
                S.dma("sp", lambda h, c=c: h.dma_start(out=self.RW[:, c, :], in_=self.xin[:, c, :]), self.d_in,
                      writes=self.bRW[c])
            S.dma("sp", lambda h: h.dma_start(out=self.CFFN[:, :, :, :, :], in_=self.cffn), self.d_in, writes=[self.bCFFN])
            S.finalize([self.bPV, self.bCFFN] + [b for c in range(NCH) for b in self.bRW[c]], self.d_in)
            S.op("pool", lambda h: h.memset(self.ONES[:, :], 1.0), writes=[self.bONES])
            cT = self.PV[:, PVO["cT"]:PVO["cT"] + 48].rearrange("p (s c) -> p c s", s=3)
            self.act(self.SCB[:, :, :], cT, AF.Silu, [self.bPV], [self.bSCB])
            self.pin([MODB])
            self.x_stats(ALLT)
        self.task(None, p0)
        self.mod_tasks(0, 0, 16, MODB)

        def p0b(_):
            self.mod_evac(0, 0, 32, MODB)
            self.mod_derive(0, 1, 0, 0, False)
            self.norm_apply(ALLT, 0, 0, 0)
            S.barrier(("pe", "act", "dve", "pool", "sp"))
            self.conv_phaseA_setup()
        self.task(None, p0b)

        for c in range(NCH):
            self.conv_chunk_task(c)
            self.mod_tasks(0, 16 + 2 * c, 18 + 2 * c, MODB)

        def p1b(_):
            self.mod_evac(0, 32, 96, MODB)
            self.unpin([MODB])
            self.mod_derive(0, 2, 1, 1, True)
            self.mod_derive(0, 4, 2, 2, False)
            self.mod_derive(0, 5, 3, 3, True)
            self.conv_mm(NCH - 1)
            S.dma("sp", lambda h: h.dma_start(out=self.o_ust, in_=self.USTATE[:, :, :]), self.d_out, reads=[self.bUSTATE])
            self.conv_ln()
            S.barrier(("pe", "act", "dve", "pool", "sp"))
        self.task(None, p1b)

        self.proj_stats_tasks(self.w_out, ALLT, "cbout", self.Y, self.bY)

        def p2b(_):
            S.barrier(("pe", "act", "dve", "pool", "sp"))
            self.resid_update(0, T, 0, 1, self.xin)
            self.x_stats(ALLT)
            self.norm_apply(ALLT, 0, 2, 3)
            S.dma("sp", lambda h: h.dma_start(out=self.rs, in_=self.RW[:, :, :]), self.d_rs,
                  reads=[b for c in range(NCH) for b in self.bRW[c]])
            S.barrier(("pe", "act", "dve", "pool", "sp"))
            self.ffn_setup(30)
            self.pin([MODB])
        self.task(None, p2b)

        def modnext(j):
            lo = (48 * j) // NJ
            hi = (48 * (j + 1)) // NJ
            self.mod_tasks(1, lo, hi, MODB)
            if j == NJ - 1:
                def fin(_):
                    self.mod_evac(1, 0, 96, MODB)
                    self.unpin([MODB])
                    self.mod_derive(1, 1, 0, 0, False)
                    self.mod_derive(1, 2, 1, 1, True)
                    self.mod_derive(1, 4, 2, 2, False)
                    self.mod_derive(1, 5, 3, 3, True)
                self.task(None, fin)
        self.ffn_tasks(0, 30, modnext)

        def p4b(_):
            S.barrier(("pe", "act", "dve", "pool", "sp"))
            self.zero_rw_head(30)
            self.resid_update(30, T, 0, 3, self.rs, fsrc=self.H)
            self.x_stats(ALLT)
            self.norm_apply(ALLT, 1, 0, 0)
            S.dma("sp", lambda h: h.dma_start(out=self.rs, in_=self.RW[:, :, :]), self.d_rs,
                  reads=[b for c in range(NCH) for b in self.bRW[c]])
            S.barrier(("pe", "act", "dve", "pool", "sp"))
            self.attn_setup()
        self.task(None, p4b)

        self.attn_proj_tasks()

        def p7(_):
            self.attn_core()
            S.barrier(("pe", "act", "dve", "pool", "sp"))
        self.task(None, p7)

        L1T = _tiles(160, T)
        self.proj_stats_tasks(self.w_o, L1T, None, self.H, self.bH)

        def p8b(_):
            S.barrier(("pe", "act", "dve", "pool", "sp"))
            self.zero_rw_head(160)
            self.resid_update(160, T, 1, 1, self.rs)
            self.x_stats(ALLT)
            self.norm_apply(ALLT, 1, 2, 3)
            S.dma("sp", lambda h: h.dma_start(out=self.rs, in_=self.RW[:, :, :]), self.d_rs,
                  reads=[b for c in range(NCH) for b in self.bRW[c]])
            S.barrier(("pe", "act", "dve", "pool", "sp"))
            self.ffn_setup(160)
        self.task(None, p8b)

        self.ffn_tasks(1, 160, None)

        def p11(_):
            S.barrier(("pe", "act", "dve", "pool", "sp"))
            def out_chunk(c):
                S.dma("sp", lambda h, c=c: h.dma_start(out=self.o_y[:, c, :], in_=self.RW[:, c, HALO:T]), self.d_out,
                      reads=self.bRW[c])
            self.resid_update(160, T, 1, 3, self.rs, fsrc=self.H, after=out_chunk)
        self.task(None, p11)

    def zero_rw_head(self, ncols):
        for c in range(NCH):
            self.S.op("dve", lambda h, c=c: h.memset(self.RW[:, c, 0:ncols], 0.0), writes=[self.bRW[c][0]])

    def conv_phaseA_setup(self):
        RWt = self.RWt
        RWb = RWt[:, :].bitcast(BF16)
        o = 0
        UW = 30 + TP + 2 * 62
        self.UW = UW
        self.UB = []
        for i in range(2):
            self.UB.append(RWb[:, 2 * o:2 * o + UW]); o += (UW + 1) // 2
        self.DG = []
        for i in range(2):
            self.DG.append(RWb[:, 2 * o:2 * o + 31 * 128].rearrange("p (k f) -> p k f", k=31)); o += 31 * 64
        self.ACC = []
        for i in range(2):
            self.ACC.append(RWt[:, o:o + T]); o += T
        self.SIG = []
        for i in range(2):
            self.SIG.append(RWt[:, o:o + 512]); o += 512
        self.USTATE = RWt[:, o:o + NCH * 90].rearrange("p (c t) -> p c t", c=NCH); o += NCH * 90
        self.CCONV = RWt[:, o:o + NCH * 60].rearrange("p (c s t) -> p c s t", c=NCH, s=2); o += NCH * 60
        self.IDF = RWt[:, o:o + 128]; o += 128
        self.IDB = RWb[:, 2 * o:2 * o + 128]; o += 64
        self.ST2 = RWt[:, o:o + T]; o += T
        assert o <= NCH * T
        self.bUB = [Buf("UB0"), Buf("UB1")]
        self.bDG = [Buf("DG0"), Buf("DG1")]
        self.bACC = [Buf("ACC0"), Buf("ACC1")]
        self.bSIG = [Buf("SIG0"), Buf("SIG1")]
        self.bUSTATE = Buf("USTATE")
        self.bCCONV = Buf("CCONV")
        self.bIDF = Buf("IDF")
        self.bIDB = Buf("IDB")
        self.sig_i = 0
        S = self.S
        S.dma("sp", lambda h: h.dma_start(out=self.CCONV[:, :, :, :], in_=self.cconv), self.d_in, writes=[self.bCCONV])
        S.dma("sp", lambda h: h.dma_start(out=self.IDF[:, :], in_=self.ident), self.d_in, writes=[self.bIDF])
        S.finalize([self.bCCONV, self.bIDF], self.d_in)
        self.cp("pool", self.IDB[:, :], self.IDF[:, :], [self.bIDF], [self.bIDB])
        for i in range(2):
            S.op("pool", lambda h, i=i: h.memset(self.UB[i][:, 0:30], 0.0), writes=[self.bUB[i]])

    def conv_mm(self, c):
        ub = c % 2
        U, bU = self.UB[ub], self.bUB[ub]
        DG, bD = self.DG[ub], self.bDG[ub]
        samp = U[:, 30 + TP:30 + TP + 124].rearrange("p (s t) -> p s t", s=2)
        for (a, b) in [(0, 512), (512, 1024), (1024, TP)]:
            bi = self.bank()
            self.mm(bi, self.PS[bi][:, 0:b - a], [(DG[:, k, :], U[:, a + k:b + k]) for k in range(31)], [bU, bD])
            self.act(self.Y[:, c, a:b], self.PS[bi][:, 0:b - a], AF.Identity, [self.bPS[bi], self.bPV], self.bufs(self.bY, c, a, b),
                     bias=self.pvc("cbdw", c))
        bi = self.bank()
        self.mm(bi, self.PS[bi][:, 0:64].rearrange("p (s t) -> p s t", s=2), [(DG[:, k, :], samp[:, :, k:k + 32]) for k in range(31)], [bU, bD])
        self.act(self.Y[:, c, TP:T], self.PS[bi][:, 0:64], AF.Identity, [self.bPS[bi], self.bPV], self.bufs(self.bY, c, TP, T),
                 bias=self.pvc("cbdw", c))

    def conv_chunk_task(self, c):
        S = self.S
        ALLT = _tiles(0, T)

        def load():
            return self.wload([(0, 256, NCH, self.w_in[:, c * 256:(c + 1) * 256])])

        def run(s):
            wt = self.WT[s][:, :].rearrange("p (c f) -> p c f", c=NCH)
            ub = c % 2
            U, bU = self.UB[ub], self.bUB[ub]
            samp = U[:, 30 + TP:30 + TP + 124].rearrange("p (s t) -> p s t", s=2)
            self.cp("pool", samp[:, :, 0:30], self.CCONV[:, c, :, :], [self.bCCONV], [bU])
            wo = PVO["cwdw"]
            for k in range(31):
                self.ts("dve", self.DG[ub][:, k, :], self.IDB[:, :], self.PV[:, wo + k * 16 + c:wo + k * 16 + c + 1], None,
                        ALU.mult, None, [self.bIDB, self.bPV], [self.bDG[ub]])
            for ti, (a, b) in enumerate(ALLT):
                w = b - a
                rd = [self.bW[s]]
                for k in range(NCH):
                    rd += self.bufs(self.bH, k, a, b)
                bg = self.bank()
                self.mm(bg, self.PS[bg][:, 0:w], [(wt[:, k, 128:256], self.H[:, k, a:b]) for k in range(NCH)], rd)
                ba = self.bank()
                self.mm(ba, self.PS[ba][:, 0:w], [(wt[:, k, 0:128], self.H[:, k, a:b]) for k in range(NCH)], rd)
                si = self.sig_i
                self.sig_i ^= 1
                self.act(self.SIG[si][:, 0:w], self.PS[bg][:, 0:w], AF.Sigmoid, [self.bPS[bg], self.bPV], [self.bSIG[si]],
                         bias=self.pvc("cbin", 16 + c))
                pb = min(b, TP)
                rdu = [self.bPS[ba], self.bSIG[si], self.bPV]
                if a < pb:
                    self.stt(U[:, 30 + a:30 + pb], self.PS[ba][:, 0:pb - a], self.pvc("cbin", c), self.SIG[si][:, 0:pb - a],
                             ALU.add, ALU.mult, rdu, [bU])
                if b > TP:
                    assert b == T and a <= 1218
                    ps3 = self.PS[ba][:, TP - a:T - a].rearrange("p (s t) -> p s t", s=2)
                    sg3 = self.SIG[si][:, TP - a:T - a].rearrange("p (s t) -> p s t", s=2)
                    self.stt(samp[:, :, 30:62], ps3, self.pvc("cbin", c), sg3, ALU.add, ALU.mult, rdu, [bU])
                    self.stt(self.USTATE[:, c, 0:30], self.PS[ba][:, 1218 - a:TP - a], self.pvc("cbin", c),
                             self.SIG[si][:, 1218 - a:TP - a], ALU.add, ALU.mult, rdu, [self.bUSTATE])
                    self.stt(self.USTATE[:, c, 30:90].rearrange("p (s t) -> p s t", s=2), ps3[:, :, 2:32], self.pvc("cbin", c),
                             sg3[:, :, 2:32], ALU.add, ALU.mult, rdu, [self.bUSTATE])
            self.ts("dve", U[:, 30 + 194:30 + 224], U[:, 30 + 194:30 + 224], self.pvc("hm", 0), None, ALU.mult, None,
                    [bU, self.bPV], [bU])
            if c > 0:
                self.conv_mm(c - 1)
        self.task(load, run)

    def conv_ln(self):
        S = self.S
        ALLT = _tiles(0, T)
        mb = [0, 1, 2]
        qb = [5, 6, 7]
        self.pin(mb + qb)
        for c in range(NCH):
            for ti, (a, b) in enumerate(ALLT):
                w = b - a
                q = self.sq_buf()
                self.act(self.SQ[q][:, 0:w], self.Y[:, c, a:b], AF.Square, self.bufs(self.bY, c, a, b), [self.bSQ[q]])
                self.stat_mm(mb[ti], w, self.Y[:, c, a:b], self.bufs(self.bY, c, a, b), first=(c == 0), last=(c == NCH - 1))
                self.stat_mm(qb[ti], w, self.SQ[q][:, 0:w], [self.bSQ[q]], first=(c == 0), last=(c == NCH - 1))
        for ti, (a, b) in enumerate(ALLT):
            w = b - a
            b1, b2 = [self.bST1[ti]], [self.bST2[ti]]
            self.act(self.ST2[:, a:b], self.PS[mb[ti]][:, 0:w], AF.Copy, [self.bPS[mb[ti]]], b2, scale=1.0 / D)
            self.tt("dve", self.ST1[:, a:b], self.ST2[:, a:b], self.ST2[:, a:b], ALU.mult, b2, b1)
            self.stt(self.ST1[:, a:b], self.PS[qb[ti]][:, 0:w], 1.0 / D, self.ST1[:, a:b], ALU.mult, ALU.subtract,
                     [self.bPS[qb[ti]]] + b1, b1)
            self.act(self.ST1[:, a:b], self.ST1[:, a:b], AF.Sqrt, b1, b1, bias=EPS)
            S.op("dve", lambda h, a=a, b=b: h.reciprocal(out=self.ST1[:, a:b], in_=self.ST1[:, a:b]), reads=b1, writes=b1)
        self.unpin(mb + qb)
        for c in range(NCH):
            k = c % 2
            A, bA = self.ACC[k], self.bACC[k]
            self.tt("dve", A[:, 0:T], self.Y[:, c, :], self.ST2[:, :], ALU.subtract, self.bY[c] + self.bST2, [bA])
            self.tt("dve", A[:, 0:T], A[:, 0:T], self.ST1[:, :], ALU.mult, [bA] + self.bST1, [bA])
            self.act(self.Y[:, c, :], A[:, 0:T], AF.Silu, [bA, self.bPV], self.bY[c],
                     scale=self.pvc("lng", c), bias=self.pvc("lnb", c))

    def ffn_setup(self, base):
        W = T - base
        RWt, Yt = self.RWt, self.Yt
        RWb = RWt[:, :].bitcast(BF16)
        n_in_y = (NCH * T) // W
        n_in_y = min(n_in_y, NJ)
        rest = NJ - n_in_y
        assert rest * W <= 2 * NCH * T
        used_f32 = (rest * W + 1) // 2
        o = used_f32
        npc = TP - base
        GW = 2 + npc + 68
        self.GEXT = RWt[:, o:o + GW]; o += GW
        self.CVb = RWt[:, o:o + W]; o += W
        self.FSTATE = RWt[:, o:o + NJ * 6].rearrange("p (j t) -> p j t", j=NJ); o += NJ * 6
        self.bFSTATE = Buf("FSTATE")
        assert o <= NCH * T, (o, NCH * T)
        self.bGEXT = Buf("GEXT")
        self.bCVb = Buf("CVb")
        self.bHID = [Buf(f"HID{j}") for j in range(NJ)]

        def HID(j):
            if j < n_in_y:
                return Yt[:, j * W:(j + 1) * W]
            jj = j - n_in_y
            return RWb[:, jj * W:(jj + 1) * W]
        self.HID = HID

    def attn_setup(self):
        RWt = self.RWt
        RWb = RWt[:, :].bitcast(BF16)
        o = 0

        def f32(n):
            nonlocal o
            v = RWt[:, o:o + n]
            o += n
            return v

        def b16(n):
            nonlocal o
            m = (n + 1) // 2
            v = RWb[:, 2 * o:2 * o + n]
            o += m
            return v
        self.KT = b16(4 * T).rearrange("p (k t) -> p k t", k=4)
        self.VT = b16(19 * 256).rearrange("p (j f) -> p j f", j=19)
        self.BT = f32(2 * 32 * 64).rearrange("p (b h q) -> p b h q", b=2, h=32)
        self.SE = f32(16 * 64).rearrange("p (h q) -> p h q", h=16)
        self.SB0 = [f32(512) for _ in range(2)]
        self.SB1 = [f32(512) for _ in range(2)]
        self.PT0 = [b16(512) for _ in range(2)]
        self.PT1 = [b16(512) for _ in range(2)]
        self.DEN = [f32(256) for _ in range(2)]
        self.KS = b16(4 * 2 * 160).rearrange("p (k s t) -> p k s t", k=4, s=2)
        self.VSC = b16(2 * 256).rearrange("p (s f) -> p s f", s=2)
        self.VSN = b16(2 * 256).rearrange("p (s f) -> p s f", s=2)
        self.KSTATE = f32(4 * 128).rearrange("p (k t) -> p k t", k=4)
        self.VSTATE = f32(256)
        self.KSNEW = f32(4 * 2 * 32).rearrange("p (k s t) -> p k s t", k=4, s=2)
        self.VSNEWF = f32(2 * 256).rearrange("p (s f) -> p s f", s=2)
        self.CK = f32(4 * 2 * 128).rearrange("p (k s t) -> p k s t", k=4, s=2)
        self.CVI = f32(2 * 256).rearrange("p (s f) -> p s f", s=2)
        self.EB = f32(255)
        self.RB = f32(32)
        self.SE16 = f32(16)
        self.EBB = b16(256)
        self.RBH = b16(32)
        self.RBL = b16(32)
        assert o <= NCH * T
        B = Buf
        self.bKT = [B(f"KT{k}") for k in range(4)]
        self.bVT = [B(f"VT{j}") for j in range(19)]
        self.bBT = B("BT"); self.bSE = B("SE")
        self.bSB0 = [B("SB00"), B("SB01")]; self.bSB1 = [B("SB10"), B("SB11")]
        self.bPT0 = [B("PT00"), B("PT01")]; self.bPT1 = [B("PT10"), B("PT11")]
        self.bDEN = [B("DEN0"), B("DEN1")]
        self.bKS = B("KS"); self.bVSC = B("VSC"); self.bVSN = B("VSN")
        self.bKSTATE = B("KSTATE"); self.bVSTATE = B("VSTATE"); self.bKSNEW = B("KSNEW"); self.bVSNEWF = B("VSNEWF")
        self.bCK = B("CK"); self.bCVI = B("CVI"); self.bEB = B("EB"); self.bRB = B("RB"); self.bSE16 = B("SE16")
        self.bEBB = B("EBB"); self.bRBH = B("RBH"); self.bRBL = B("RBL")
        S = self.S
        S.dma("sp", lambda h: h.dma_start(out=self.CK[:, :, :, :], in_=self.ck), self.d_in, writes=[self.bCK])
        S.dma("sp", lambda h: h.dma_start(out=self.CVI[:, :, :], in_=self.cv), self.d_in, writes=[self.bCVI])
        S.dma("sp", lambda h: h.dma_start(out=self.EB[0:32, :], in_=self.ebase), self.d_in, writes=[self.bEB])
        S.dma("sp", lambda h: h.dma_start(out=self.RB[0:32, :], in_=self.relb), self.d_in, writes=[self.bRB])
        S.finalize([self.bCK, self.bCVI, self.bEB, self.bRB], self.d_in)
        S.dma("sp", lambda h: h.dma_start(out=self.o_kss[:, :, :, 0:96], in_=self.CK[0:64, :, :, 32:128]), self.d_out,
              reads=[self.bCK])
        for s in range(2):
            S.dma("sp", lambda h, s=s: h.dma_start(out=self.o_vss[s, 0:96, :], in_=self.CVI[32:128, s, :]), self.d_out,
                  reads=[self.bCVI])
        self.ts("dve", self.KMB[:, 34:35], self.pvc("hm", 0), -1.0, 30000.0, ALU.add, ALU.mult, [self.bPV], [self.bKMB])
        self.ts("dve", self.KMB[:, 0:34], self.PV[:, PVO["ind"]:PVO["ind"] + 34], self.KMB[:, 34:35], None, ALU.mult, None,
                [self.bPV, self.bKMB], [self.bKMB])
        self.act(self.SE16[:, :], self.PV[:, PVO["sink"]:PVO["sink"] + 16], AF.Exp, [self.bPV], [self.bSE16])
        self.cp("dve", self.SE[:, :, :], self.SE16[:, :].unsqueeze(2).broadcast_to([128, 16, 64]), [self.bSE16], [self.bSE])
        self.cp("pool", self.KS[:, :, :, 0:128], self.CK[:, :, :, :], [self.bCK], [self.bKS])
        self.cp("pool", self.VSC[:, :, :], self.CVI[:, :, :], [self.bCVI], [self.bVSC])
        self.cp("dve", self.EBB[0:32, 0:255], self.EB[0:32, :], [self.bEB], [self.bEBB])
        self.cp("dve", self.RBH[0:32, :], self.RB[0:32, :], [self.bRB], [self.bRBH])
        self.tt("dve", self.RB[0:32, :], self.RB[0:32, :], self.RBH[0:32, :], ALU.subtract, [self.bRB, self.bRBH], [self.bRB])
        self.cp("dve", self.RBL[0:32, :], self.RB[0:32, :], [self.bRB], [self.bRBL])
        for blk in range(2):
            M = 128 if blk == 0 else 64
            for qg in range(4):
                bi = self.bank()
                for qi in range(16):
                    q = qg * 16 + qi
                    off = blk * 128 + 63 - q
                    for part in range(2):
                        rr = self.RBH if part == 0 else self.RBL
                        self.S.op("pe", lambda h, bi=bi, qi=qi, off=off, M=M, rr=rr, part=part: h.matmul(
                            self.PS[bi][0:M, qi * 32:(qi + 1) * 32], lhsT=self.EBB[0:32, off:off + M], rhs=rr[0:32, :],
                            start=(part == 0), stop=(part == 1)),
                            reads=[self.bEBB, self.bRBH, self.bRBL] if (qi == 0 and part == 0) else (),
                            writes=[self.bPS[bi]] if (qi == 0 and part == 0) else (),
                            inc=(qi == 15 and part == 1), skip_self=True)
                self.cp("dve", self.BT[0:M, blk, :, qg * 16:(qg + 1) * 16],
                        self.PS[bi][0:M, 0:512].rearrange("p (q h) -> p h q", q=16), [self.bPS[bi]], [self.bBT])

    def attn_proj_tasks(self):
        S = self.S
        KT_T = _tiles(32, T)
        L1T = _tiles(160, T)

        def hreads(a, b):
            rd = []
            for c in range(NCH):
                rd += self.bufs(self.bH, c, a, b)
            return rd

        def mk_k(kb):
            def load():
                return self.wload([(0, 256, NCH, self.w_k[:, kb * 256:(kb + 1) * 256])])

            def run(s):
                wt = self.WT[s][:, :].rearrange("p (c f) -> p c f", c=NCH)
                for kk in range(2):
                    kvh = kb * 2 + kk
                    for (a, b) in KT_T:
                        w = b - a
                        bi = self.bank()
                        self.mm(bi, self.PS[bi][:, 0:w], [(wt[:, c, kk * 128:(kk + 1) * 128], self.H[:, c, a:b]) for c in range(NCH)],
                                [self.bW[s]] + hreads(a, b))
                        self.act(self.KT[:, kvh, a:b], self.PS[bi][:, 0:w], AF.Copy, [self.bPS[bi]], [self.bKT[kvh]])
                        if b == T:
                            assert a <= 1120
                            self.cp("dve", self.KSTATE[0:64, kvh, :], self.PS[bi][0:64, 1120 - a:1248 - a], [self.bPS[bi]], [self.bKSTATE])
                            self.cp("dve", self.KSNEW[0:64, kvh, :, :], self.PS[bi][0:64, TP - a:T - a].rearrange("p (s t) -> p s t", s=2),
                                    [self.bPS[bi]], [self.bKSNEW])
                    self.cp("pool", self.KS[:, kvh, :, 128:160], self.KT[:, kvh, TP:T].rearrange("p (s t) -> p s t", s=2),
                            [self.bKT[kvh]], [self.bKS])
                if kb == 1:
                    S.dma("sp", lambda h: h.dma_start(out=self.o_kst, in_=self.KSTATE[0:64, :, :]), self.d_out, reads=[self.bKSTATE])
                    S.dma("sp", lambda h: h.dma_start(out=self.o_kss[:, :, :, 96:128], in_=self.KSNEW[0:64, :, :, :]), self.d_out,
                          reads=[self.bKSNEW])
            return load, run
        for kb in range(2):
            ld, rn = mk_k(kb)
            self.task(ld, rn)

        def load_v():
            return self.wload([(0, 256, NCH, self.w_v[:, :])])

        def run_v(s):
            wt = self.WT[s][:, :].rearrange("p (c f) -> p c f", c=NCH)
            for j in range(19):
                t0 = 32 + 64 * j
                bi = self.bank()
                self.mm(bi, self.PS[bi][:, 0:256], [(self.H[:, c, t0:t0 + 128], wt[:, c, :]) for c in range(NCH)],
                        [self.bW[s]] + hreads(t0, t0 + 128))
                self.act(self.VT[:, j, :], self.PS[bi][:, 0:256], AF.Copy, [self.bPS[bi]], [self.bVT[j]])
                if j == 17:
                    self.cp("dve", self.VSTATE[:, :], self.PS[bi][:, 0:256], [self.bPS[bi]], [self.bVSTATE])
                    S.dma("sp", lambda h: h.dma_start(out=self.o_vst, in_=self.VSTATE[:, :]), self.d_out, reads=[self.bVSTATE])
            for sq in range(2):
                t0 = TP + 32 * sq
                bi = self.bank()
                self.mm(bi, self.PS[bi][0:32, 0:256], [(self.H[:, c, t0:t0 + 32], wt[:, c, :]) for c in range(NCH)],
                        [self.bW[s]] + hreads(t0, t0 + 32))
                self.act(self.VSN[0:32, sq, :], self.PS[bi][0:32, 0:256], AF.Copy, [self.bPS[bi]], [self.bVSN])
                self.cp("dve", self.VSNEWF[0:32, sq, :], self.PS[bi][0:32, 0:256], [self.bPS[bi]], [self.bVSNEWF])
            for sq in range(2):
                S.dma("sp", lambda h, sq=sq: h.dma_start(out=self.o_vss[sq, 96:128, :], in_=self.VSNEWF[0:32, sq, :]), self.d_out,
                      reads=[self.bVSNEWF])
        self.task(load_v, run_v)

        def mk_q(blk):
            def load():
                return self.wload([(0, 256, NCH, self.w_q[:, blk * 256:(blk + 1) * 256])])

            def run(s):
                wt = self.WT[s][:, :].rearrange("p (c f) -> p c f", c=NCH)
                for ii in range(2):
                    i = blk * 2 + ii
                    for (a, b) in L1T:
                        w = b - a
                        bi = self.bank()
                        self.mm(bi, self.PS[bi][:, 0:w], [(wt[:, c, ii * 128:(ii + 1) * 128], self.H[:, c, a:b]) for c in range(NCH)],
                                [self.bW[s]] + hreads(a, b))
                        self.act(self.Y[:, i, a:b], self.PS[bi][:, 0:w], AF.Copy, [self.bPS[bi]], self.bufs(self.bY, i, a, b), scale=0.125)
            return load, run
        for blk in range(8):
            ld, rn = mk_q(blk)
            self.task(ld, rn)

    def attn_core(self):
        S = self.S
        jobs = [("p", n) for n in range(17)] + [("s", sq) for sq in range(2)]
        items = [(job, kvh) for job in jobs for kvh in range(4)]

        def stage_a(it, job, kvh):
            k = it % 2
            if job[0] == "p":
                n = job[1]
                q0 = 160 + 64 * n
                nq = 64
                kt0 = [self.KT[64 * hf:64 * hf + 64, kvh, q0 - 128:q0] for hf in range(2)]
                kt1 = [self.KT[64 * hf:64 * hf + 64, kvh, q0:q0 + 64] for hf in range(2)]
                M1 = 64
                v0 = self.VT[:, n, kvh * 64:(kvh + 1) * 64]
                v1 = self.VT[0:64, n + 2, kvh * 64:(kvh + 1) * 64]
                rv = [self.bVT[n], self.bVT[n + 2]]
                rk = [self.bKT[kvh]]
                qsl = [self.Y[64 * hf:64 * hf + 64, 4 * kvh:4 * kvh + 4, q0:q0 + 64] for hf in range(2)]
                m0 = self.KMB[:, 2 * n:2 * n + 1]
                m1 = self.KMB[0:64, 2 * n + 1:2 * n + 2]
            else:
                sq = job[1]
                q0 = TP + 32 * sq
                nq = 32
                kt0 = [self.KS[64 * hf:64 * hf + 64, kvh, sq, 0:128] for hf in range(2)]
                kt1 = [self.KS[64 * hf:64 * hf + 64, kvh, sq, 128:160] for hf in range(2)]
                M1 = 32
                v0 = self.VSC[:, sq, kvh * 64:(kvh + 1) * 64]
                v1 = self.VSN[0:32, sq, kvh * 64:(kvh + 1) * 64]
                rv = [self.bVSC, self.bVSN]
                rk = [self.bKS]
                qsl = [self.Y[64 * hf:64 * hf + 64, 4 * kvh:4 * kvh + 4, q0:q0 + 32] for hf in range(2)]
                m0 = 0.0
                m1 = 0.0
            rq = []
            for c in range(4 * kvh, 4 * kvh + 4):
                rq += self.bufs(self.bY, c, q0, q0 + nq)
            ocols = (q0, q0 + nq)
            NC4 = 4 * nq
            NCOL = 8 * nq
            bx = [self.bank(), self.bank()]
            for half in range(2):
                S.op("pe", lambda h, half=half, bx=bx, kt0=kt0, qsl=qsl, NC4=NC4: h.matmul(
                    self.PS[bx[half]][:, 0:NC4], lhsT=kt0[half], rhs=qsl[half], start=True, stop=True),
                    reads=(rk + rq), writes=[self.bPS[bx[half]]], inc=False, skip_self=True)
                S.op("pe", lambda h, half=half, bx=bx, kt1=kt1, qsl=qsl, M1=M1, NC4=NC4: h.matmul(
                    self.PS[bx[half]][0:M1, NC4:2 * NC4], lhsT=kt1[half], rhs=qsl[half], start=True, stop=True),
                    reads=(), writes=(), inc=True, skip_self=True)
            for half in range(2):
                hs = kvh * 8 + 4 * half
                self.tt("dve", self.SB0[k][:, half * NC4:(half + 1) * NC4].rearrange("p (h q) -> p h q", h=4),
                        self.PS[bx[half]][:, 0:NC4].rearrange("p (h q) -> p h q", h=4), self.BT[:, 0, hs:hs + 4, 0:nq], ALU.add,
                        [self.bPS[bx[half]], self.bBT], [self.bSB0[k]])
                self.tt("dve", self.SB1[k][0:M1, half * NC4:(half + 1) * NC4].rearrange("p (h q) -> p h q", h=4),
                        self.PS[bx[half]][0:M1, NC4:2 * NC4].rearrange("p (h q) -> p h q", h=4), self.BT[0:M1, 1, hs:hs + 4, 0:nq], ALU.add,
                        [self.bPS[bx[half]], self.bBT], [self.bSB1[k]])
            self.act(self.PT0[k][:, 0:NCOL], self.SB0[k][:, 0:NCOL], AF.Exp, [self.bSB0[k], self.bKMB], [self.bPT0[k]], bias=m0)
            self.act(self.PT1[k][0:M1, 0:NCOL], self.SB1[k][0:M1, 0:NCOL], AF.Exp, [self.bSB1[k], self.bKMB], [self.bPT1[k]],
                     bias=(m1 if isinstance(m1, float) else m1[0:M1, :]))
            return dict(k=k, kvh=kvh, nq=nq, M1=M1, v0=v0, v1=v1, rv=rv, ocols=ocols, NC4=NC4)

        def stage_b(cx):
            k, kvh, nq, M1, v0, v1, rv, ocols, NC4 = (cx[x] for x in ("k", "kvh", "nq", "M1", "v0", "v1", "rv", "ocols", "NC4"))
            bo = self.bank(); bs = self.bank()
            for half in range(2):
                for blk in range(2):
                    lhs = v0 if blk == 0 else v1
                    rhs = (self.PT0[k][:, half * NC4:(half + 1) * NC4] if blk == 0
                           else self.PT1[k][0:M1, half * NC4:(half + 1) * NC4])
                    first = (half == 0 and blk == 0)
                    S.op("pe", lambda h, lhs=lhs, rhs=rhs, half=half, blk=blk, bo=bo, NC4=NC4: h.matmul(
                        self.PS[bo][64 * half:64 * half + 64, 0:NC4], lhsT=lhs, rhs=rhs, start=(blk == 0), stop=(blk == 1)),
                        reads=(rv + [self.bPT0[k], self.bPT1[k]]) if first else (), writes=[self.bPS[bo]] if first else (),
                        inc=(half == 1 and blk == 1), skip_self=True)
            for half in range(2):
                for blk in range(2):
                    lhs = self.ONES[:, 0:64] if blk == 0 else self.ONES[0:M1, 0:64]
                    rhs = (self.PT0[k][:, half * NC4:(half + 1) * NC4] if blk == 0
                           else self.PT1[k][0:M1, half * NC4:(half + 1) * NC4])
                    first = (half == 0 and blk == 0)
                    S.op("pe", lambda h, lhs=lhs, rhs=rhs, half=half, blk=blk, bs=bs, NC4=NC4: h.matmul(
                        self.PS[bs][64 * half:64 * half + 64, 0:NC4], lhsT=lhs, rhs=rhs, start=(blk == 0), stop=(blk == 1)),
                        reads=[self.bPT0[k], self.bPT1[k], self.bONES] if first else (), writes=[self.bPS[bs]] if first else (),
                        inc=(half == 1 and blk == 1), skip_self=True)
            den = self.DEN[k][:, 0:NC4].rearrange("p (h q) -> p h q", h=4)
            self.tt("dve", den, self.PS[bs][:, 0:NC4].rearrange("p (h q) -> p h q", h=4),
                    self.SE[:, kvh * 4:(kvh + 1) * 4, 0:nq], ALU.add, [self.bPS[bs], self.bSE], [self.bDEN[k]])
            S.op("dve", lambda h, k=k, NC4=NC4: h.reciprocal(out=self.DEN[k][:, 0:NC4], in_=self.DEN[k][:, 0:NC4]),
                 reads=[self.bDEN[k]], writes=[self.bDEN[k]])
            wb = []
            for c in range(4 * kvh, 4 * kvh + 4):
                wb += self.bufs(self.bH, c, ocols[0], ocols[1])
            self.tt("dve", self.H[:, 4 * kvh:4 * kvh + 4, ocols[0]:ocols[1]],
                    self.PS[bo][:, 0:NC4].rearrange("p (h q) -> p h q", h=4), den, ALU.mult,
                    [self.bPS[bo], self.bDEN[k]], wb)

        prev = None
        for it, (job, kvh) in enumerate(items):
            cx = stage_a(it, job, kvh)
            if prev is not None:
                stage_b(prev)
            prev = cx
        stage_b(prev)


_NC_CACHE = {}


def _get_nc():
    if "nc" not in _NC_CACHE:
        _NC_CACHE["nc"] = Prog().build()
    return _NC_CACHE["nc"]


def _fm(v):
    v = np.asarray(v, np.float32)
    lead = v.shape[:-1]
    n = v.shape[-1] // 128
    v = v.reshape(lead + (n, 128))
    return np.moveaxis(v, -1, 0)


def _prep_shared(inp):
    f = lambda k: np.asarray(inp[k], np.float32)
    sh = {}
    sh["w_mod"] = f("w_mod")
    w_in = f("conv_w_in")[0]
    sh["w_in"] = np.ascontiguousarray(
        np.concatenate([w_in[:, :D].reshape(D, NCH, 1, 128), w_in[:, D:].reshape(D, NCH, 1, 128)], axis=2).reshape(D, 2 * D))
    sh["w_out"] = f("conv_w_out")[0]
    sh["w_q"] = f("attn_w_q")[0]
    wk = f("attn_w_k")[0].reshape(D, 4, 1, 64)
    sh["w_k"] = np.ascontiguousarray(np.concatenate([wk, wk], axis=2).reshape(D, 512))
    sh["w_v"] = f("attn_w_v")[0]
    sh["w_o"] = f("attn_w_o")[0]
    w_up = f("ffn_w_up")
    sh["w_up"] = np.ascontiguousarray(
        np.concatenate([w_up[:, :, :DFF].reshape(2, D, NJ, 1, 128), w_up[:, :, DFF:].reshape(2, D, NJ, 1, 128)], axis=3
                       ).reshape(2, D, 2 * DFF))
    sh["w_down"] = f("ffn_w_down")
    rb = f("rel_bias")
    order = [kvh * 8 + 2 * g + par for kvh in range(4) for par in range(2) for g in range(4)]
    sh["relb"] = np.ascontiguousarray(rb[:, order])
    r = np.arange(255) - 191
    bk = _t5_bucket(r.astype(np.int32))
    eb = np.zeros((32, 255), np.float32)
    eb[bk, np.arange(255)] = 1.0
    sh["ebase"] = eb
    sh["ident"] = np.eye(128, dtype=np.float32)
    return sh


def _prep_core(inp, core):
    f = lambda k: np.asarray(inp[k], np.float32)
    b, seg = core // 4, core % 4
    s0 = seg * NREAL
    xp = f("x_prompt")[b]
    X = np.zeros((T, D), np.float32)
    if seg > 0:
        X[0:HALO] = xp[s0 - HALO:s0]
    X[HALO:TP] = xp[s0:s0 + NREAL]
    X[TP:T] = f("x_sample")[2 * core:2 * core + 2].reshape(64, D)
    m = {}
    m["xin"] = np.ascontiguousarray(X.T.reshape(NCH, 128, T).transpose(1, 0, 2))
    pv = np.zeros((128, NV), np.float32)

    def put(name, arr):
        arr = np.asarray(arr, np.float32).reshape(128, -1)
        pv[:, PVO[name]:PVO[name] + arr.shape[1]] = arr
    put("gnorm", _fm(f("g_norm").reshape(8, D)))
    put("bmod", _fm(f("b_mod")))
    put("cbin", _fm(f("conv_b_in")[0]))
    put("cwdw", _fm(f("conv_w_dw")[0]))
    put("cbdw", _fm(f("conv_b_dw")[0]))
    put("lng", _fm(f("conv_ln_g")[0]))
    put("lnb", _fm(f("conv_ln_b")[0]))
    put("cbout", _fm(f("conv_b_out")[0]))
    put("fwdw", _fm(f("ffn_w_dw")))
    put("fbdw", _fm(f("ffn_b_dw")))
    cs = np.stack([f("c_prompt")[b], f("c_sample")[2 * core], f("c_sample")[2 * core + 1]])
    put("cT", _fm(cs))
    pv[:, PVO["hm"]] = 1.0 if seg > 0 else 0.0
    sinks = f("attn_sinks")[0]
    sl = np.zeros((128, 16), np.float32)
    for kvh in range(4):
        for g in range(4):
            sl[0:64, kvh * 4 + g] = sinks[kvh * 8 + 2 * g]
            sl[64:128, kvh * 4 + g] = sinks[kvh * 8 + 2 * g + 1]
    put("sink", sl)
    ind = np.zeros((128, 34), np.float32)
    p = np.arange(128)
    for n in range(17):
        q0 = 160 + 64 * n
        ind[:, 2 * n] = ((q0 - 128 + p) < HALO)
        ind[:, 2 * n + 1] = ((q0 + p) < HALO)
    put("ind", ind)
    m["pvec"] = pv
    cc = f("cache_conv")[0, 2 * core:2 * core + 2]
    m["cconv"] = np.ascontiguousarray(cc.reshape(2, 30, NCH, 128).transpose(3, 2, 0, 1))
    ck = f("cache_k")[0, 2 * core:2 * core + 2]
    ckt = ck.transpose(3, 2, 0, 1)
    m["ck"] = np.ascontiguousarray(np.concatenate([ckt, ckt], axis=0))
    cv = f("cache_v")[0, 2 * core:2 * core + 2]
    m["cv"] = np.ascontiguousarray(cv.reshape(2, 128, 256).transpose(1, 0, 2))
    cf = f("cache_ffn")[:, 2 * core:2 * core + 2]
    m["cffn"] = np.ascontiguousarray(cf.reshape(2, 2, 2, NJ, 128).transpose(4, 0, 3, 1, 2))
    return m


def _run(inp, cores=None):
    nc = _get_nc()
    sh = _prep_shared(inp)
    cores = list(range(NCORES)) if cores is None else cores
    in_maps = []
    for c in cores:
        m = _prep_core(inp, c)
        m.update(sh)
        in_maps.append(m)
    res = run_bass_kernel_spmd(nc, in_maps, core_ids=list(range(len(cores))))
    return res.results


def _assemble(inp, results):
    B, SEQ, DB = 2, 4096, 16
    y_p = np.zeros((B, SEQ, D), np.float32)
    y_s = np.zeros((DB, 32, D), np.float32)
    cs_p = np.zeros((1, B, 30, D), np.float32)
    cs_s = np.zeros((1, DB, 30, D), np.float32)
    k_p = np.zeros((1, B, 128, 4, 64), np.float32)
    v_p = np.zeros((1, B, 128, 4, 64), np.float32)
    k_s = np.zeros((1, DB, 128, 4, 64), np.float32)
    v_s = np.zeros((1, DB, 128, 4, 64), np.float32)
    f_p = np.zeros((2, B, 2, DFF), np.float32)
    f_s = np.zeros((2, DB, 2, DFF), np.float32)
    for core, r in enumerate(results):
        b, seg = core // 4, core % 4
        oy = np.asarray(r["o_y"])
        yt = oy.transpose(2, 1, 0).reshape(NOUT, D)
        y_p[b, seg * NREAL:(seg + 1) * NREAL] = yt[:NREAL]
        y_s[2 * core:2 * core + 2] = yt[NREAL:].reshape(2, 32, D)
        ust = np.asarray(r["o_ust"]).transpose(2, 1, 0).reshape(90, D)
        cs_s[0, 2 * core] = ust[30:60]
        cs_s[0, 2 * core + 1] = ust[60:90]
        kss = np.asarray(r["o_kss"])
        k_s[0, 2 * core:2 * core + 2] = kss.transpose(2, 3, 1, 0)
        vss = np.asarray(r["o_vss"])
        v_s[0, 2 * core:2 * core + 2] = vss.reshape(2, 128, 4, 64)
        fst = np.asarray(r["o_fst"])
        ft = fst.transpose(1, 3, 2, 0).reshape(2, 6, DFF)
        f_s[:, 2 * core] = ft[:, 2:4]
        f_s[:, 2 * core + 1] = ft[:, 4:6]
        if seg == 3:
            cs_p[0, b] = ust[0:30]
            k_p[0, b] = np.asarray(r["o_kst"]).transpose(2, 1, 0)
            v_p[0, b] = np.asarray(r["o_vst"]).reshape(128, 4, 64)
            f_p[:, b] = ft[:, 0:2]
    return (y_p, y_s, cs_p, cs_s, k_p, v_p, k_s, v_s, f_p, f_s)


def kernel(**inputs):
    results = _run(inputs)
    return _assemble(inputs, results)
```
